# Optimizing a Trainium2 kernel written in Bass

```python
import math
import jax, jax.numpy as jnp
from jax import lax
import numpy as np

D_MODEL = 1024
BATCH = 4
SEQ = 4096
DEPTH = 2

BRANCH_WIDTH = D_MODEL // 2
N_BRANCHES = 3
EPS = 1e-6

A_HEADS = 4
A_DK = BRANCH_WIDTH // A_HEADS
A_DV = BRANCH_WIDTH // A_HEADS
A_CHUNK = 64

B_HEADS = 4
B_DK = BRANCH_WIDTH // B_HEADS
B_DV = BRANCH_WIDTH // B_HEADS
B_CONV = 4
B_CONV_CH = B_HEADS * (2 * B_DK + B_DV)
B_CHUNK = 64

C_Q_HEADS = 8
C_KV_HEADS = 2
C_GROUP = C_Q_HEADS // C_KV_HEADS
C_HEAD_DIM = BRANCH_WIDTH // C_Q_HEADS
WINDOW = 128
C_BLOCK = 128
N_BUCKETS = 32
MAX_DISTANCE = 128

IN_WIDTHS = (
    A_HEADS * A_DK, A_HEADS * A_DK, A_HEADS * A_DV, A_HEADS * A_DV,
    B_CONV_CH, B_HEADS * B_DV, B_HEADS, B_HEADS,
    C_Q_HEADS * C_HEAD_DIM, C_KV_HEADS * C_HEAD_DIM,
    C_KV_HEADS * C_HEAD_DIM, C_Q_HEADS * C_HEAD_DIM,
    N_BRANCHES * D_MODEL,
)
N_IN = sum(IN_WIDTHS)

kernel_name = 'hybrid_hgrn2_gdn_swa_sink_gated_merge'

F32 = jnp.float32


def _rmsnorm(x, w):
    x32 = x.astype(F32)
    y = x32 * lax.rsqrt(jnp.mean(x32 * x32, axis=-1, keepdims=True) + EPS)
    return (y * w.astype(F32)).astype(x.dtype)


def _gated_rmsnorm(o, gate, w):
    y = o * lax.rsqrt(jnp.mean(o * o, axis=-1, keepdims=True) + EPS) * w.astype(F32)
    return y * jax.nn.silu(gate)


def _l2norm(t):
    return t * lax.rsqrt(jnp.sum(t * t, axis=-1, keepdims=True) + EPS)


def _causal_conv(x, w):
    return lax.conv_general_dilated(
        x, w[:, None, :].astype(x.dtype), window_strides=(1,), padding=[(B_CONV - 1, 0)],
        dimension_numbers=('NWC', 'WIO', 'NWC'), feature_group_count=x.shape[-1])


def _t5_bucket(dist):
    max_exact = N_BUCKETS // 2
    d_f = jnp.maximum(dist, 1).astype(F32)
    large = max_exact + (jnp.log(d_f / max_exact) / math.log(MAX_DISTANCE / max_exact)
                         * (N_BUCKETS - max_exact)).astype(jnp.int32)
    large = jnp.minimum(large, N_BUCKETS - 1)
    return jnp.where(dist < max_exact, dist, large)


def _band_dist():
    i = jnp.arange(C_BLOCK)[:, None]
    j = jnp.arange(2 * C_BLOCK)[None, :]
    return i + C_BLOCK - j


def _band_bias(rel_bias):
    bucket = _t5_bucket(jnp.maximum(_band_dist(), 0))
    bias = rel_bias.astype(F32)[bucket]
    return bias.transpose(2, 0, 1).reshape(C_KV_HEADS, C_GROUP, C_BLOCK, 2 * C_BLOCK)


def _hgrn2_branch(q, f, i, g, lb, norm_w):
    Bn, S, _ = q.shape
    n = S // A_CHUNK
    q = jax.nn.silu(q.astype(F32))
    z = f.astype(F32)
    log_f = jnp.logaddexp(jnp.log(lb), jnp.log1p(-lb) + jax.nn.log_sigmoid(z))
    k = (1.0 - lb) * jax.nn.sigmoid(-z)

    def chunks(t, d):
        return t.reshape(Bn, n, A_CHUNK, A_HEADS, d).transpose(1, 0, 3, 2, 4)

    qc, kc, gc = chunks(q, A_DK), chunks(k, A_DK), chunks(log_f, A_DK)
    vc = chunks(i.astype(F32), A_DV)
    causal = jnp.tril(jnp.ones((A_CHUNK, A_CHUNK), dtype=bool))[:, :, None]

    def step(state, inp):
        qb, kb, vb, gb = inp
        b = jnp.cumsum(gb, axis=2)
        diff = b[:, :, :, None, :] - b[:, :, None, :, :]
        decay = jnp.exp(jnp.where(causal, diff, -jnp.inf))
        scores = jnp.einsum('bhtk,bhsk,bhtsk->bhts', qb, kb, decay)
        o = (jnp.einsum('bhts,bhsv->bhtv', scores, vb)
             + jnp.einsum('bhtk,bhkv->bhtv', qb * jnp.exp(b), state))
        b_end = b[:, :, -1:, :]
        state = (state * jnp.exp(b_end[:, :, 0, :, None])
                 + jnp.einsum('bhsk,bhsv->bhkv', kb * jnp.exp(b_end - b), vb))
        return state, o

    state0 = jnp.zeros((Bn, A_HEADS, A_DK, A_DV), F32)
    _, o = lax.scan(step, state0, (qc, kc, vc, gc))
    o = o.transpose(1, 0, 3, 2, 4).reshape(Bn, S, A_HEADS, A_DV)
    gate = g.astype(F32).reshape(Bn, S, A_HEADS, A_DV)
    return _gated_rmsnorm(o, gate, norm_w).reshape(Bn, S, A_HEADS * A_DV)


def _gated_deltanet_branch(qkv, z, beta_logit, a_logit, conv_w, a_log, dt_bias, norm_w):
    Bn, S, _ = qkv.shape
    n = S // B_CHUNK
    C = B_CHUNK
    qkv = jax.nn.silu(_causal_conv(qkv, conv_w)).astype(F32)
    q, k, v = jnp.split(qkv, [B_HEADS * B_DK, 2 * B_HEADS * B_DK], axis=-1)
    q = _l2norm(q.reshape(Bn, S, B_HEADS, B_DK)) * (B_DK ** -0.5)
    k = _l2norm(k.reshape(Bn, S, B_HEADS, B_DK))
    v = v.reshape(Bn, S, B_HEADS, B_DV)
    beta = jax.nn.sigmoid(beta_logit.astype(F32))
    g = -jnp.exp(a_log.astype(F32)) * jax.nn.softplus(a_logit.astype(F32) + dt_bias.astype(F32))

    def chunks4(t):
        return t.reshape(Bn, n, C, B_HEADS, t.shape[-1]).transpose(1, 0, 3, 2, 4)

    def chunks3(t):
        return t.reshape(Bn, n, C, B_HEADS).transpose(1, 0, 3, 2)

    qc, kc, vc = chunks4(q), chunks4(k), chunks4(v)
    bc, gcum = chunks3(beta), jnp.cumsum(chunks3(g), axis=-1)
    incl = jnp.tril(jnp.ones((C, C), dtype=bool))
    strict = jnp.tril(jnp.ones((C, C), dtype=bool), k=-1)
    L = jnp.exp(jnp.where(incl, gcum[..., :, None] - gcum[..., None, :], -jnp.inf))
    kb = kc * bc[..., None]
    A = jnp.where(strict, jnp.einsum('nbhtd,nbhsd->nbhts', kb, kc) * L, 0.0)
    eye = jnp.eye(C, dtype=F32)
    T = lax.linalg.triangular_solve(eye + A, jnp.broadcast_to(eye, A.shape),
                                    left_side=True, lower=True, unit_diagonal=True)
    u = T @ (vc * bc[..., None])
    w = T @ (kb * jnp.exp(gcum)[..., None])
    qk = jnp.where(incl, jnp.einsum('nbhtd,nbhsd->nbhts', qc, kc) * L, 0.0)
    q_dec = qc * jnp.exp(gcum)[..., None]
    k_dec = kc * jnp.exp(gcum[..., -1:] - gcum)[..., None]
    g_end = jnp.exp(gcum[..., -1])

    def step(state, inp):
        qk_c, u_c, w_c, qd_c, kd_c, ge_c = inp
        v_new = u_c - jnp.einsum('bhtk,bhkv->bhtv', w_c, state)
        o = (jnp.einsum('bhtk,bhkv->bhtv', qd_c, state)
             + jnp.einsum('bhts,bhsv->bhtv', qk_c, v_new))
        state = state * ge_c[..., None, None] + jnp.einsum('bhsk,bhsv->bhkv', kd_c, v_new)
        return state, o

    state0 = jnp.zeros((Bn, B_HEADS, B_DK, B_DV), F32)
    _, o = lax.scan(step, state0, (qk, u, w, q_dec, k_dec, g_end))
    o = o.transpose(1, 0, 3, 2, 4).reshape(Bn, S, B_HEADS, B_DV)
    gate = z.astype(F32).reshape(Bn, S, B_HEADS, B_DV)
    return _gated_rmsnorm(o, gate, norm_w).reshape(Bn, S, B_HEADS * B_DV)


def _swa_sink_branch(q, k, v, g, sinks, bias_blk):
    Bn, S, _ = q.shape
    nb = S // C_BLOCK
    qb = q.astype(F32).reshape(Bn, nb, C_BLOCK, C_KV_HEADS, C_GROUP, C_HEAD_DIM)

    def band(t):
        t = t.astype(F32).reshape(Bn, S, C_KV_HEADS, C_HEAD_DIM)
        t = jnp.pad(t, ((0, 0), (C_BLOCK, 0), (0, 0), (0, 0)))
        t = t.reshape(Bn, nb + 1, C_BLOCK, C_KV_HEADS, C_HEAD_DIM)
        return jnp.concatenate([t[:, :-1], t[:, 1:]], axis=2)

    kb, vb = band(k), band(v)
    logits = jnp.einsum('bnqhgd,bnkhd->bnhgqk', qb, kb) * (C_HEAD_DIM ** -0.5) + bias_blk
    dist = _band_dist()
    key_pos = jnp.arange(nb)[:, None] * C_BLOCK + jnp.arange(2 * C_BLOCK)[None, :] - C_BLOCK
    mask = ((dist >= 0) & (dist < WINDOW))[None, :, :] & (key_pos >= 0)[:, None, :]
    logits = jnp.where(mask[None, :, None, None], logits, -jnp.inf)
    sink = jnp.broadcast_to(sinks.astype(F32).reshape(C_KV_HEADS, C_GROUP)[:, :, None, None],
                            logits.shape[:-1] + (1,))
    probs = jax.nn.softmax(jnp.concatenate([logits, sink], axis=-1), axis=-1)[..., :-1]
    o = jnp.einsum('bnhgqk,bnkhd->bnqhgd', probs, vb).reshape(Bn, S, C_Q_HEADS * C_HEAD_DIM)
    return o * jax.nn.silu(g.astype(F32))


def setup_inputs(seed: int = 0) -> dict:
    key = jax.random.key(seed)
    ks = jax.random.split(key, 15)
    nrm = jax.random.normal
    x = nrm(ks[0], (BATCH, SEQ, D_MODEL), F32)
    norm_w = 1.0 + 0.02 * nrm(ks[1], (DEPTH, D_MODEL), F32)
    w_in = nrm(ks[2], (DEPTH, D_MODEL, N_IN), F32) * (D_MODEL ** -0.5)
    conv_w = nrm(ks[3], (DEPTH, B_CONV, B_CONV_CH), F32) * (B_CONV ** -0.5)
    a_log = jnp.log(jax.random.uniform(ks[4], (DEPTH, B_HEADS), F32, 1.0, 16.0))
    dt = jnp.exp(jax.random.uniform(ks[5], (DEPTH, B_HEADS), F32, math.log(1e-3), math.log(0.1)))
    dt_bias = dt + jnp.log(-jnp.expm1(-dt))
    lb_param = 0.1 * nrm(ks[6], (DEPTH, A_HEADS * A_DK), F32)
    norm_a = 1.0 + 0.02 * nrm(ks[7], (DEPTH, A_DV), F32)
    norm_b = 1.0 + 0.02 * nrm(ks[8], (DEPTH, B_DV), F32)
    sinks = 0.5 * nrm(ks[9], (DEPTH, C_Q_HEADS), F32)
    rel_bias = 0.5 * nrm(ks[10], (N_BUCKETS, C_Q_HEADS), F32)
    w_branch = nrm(ks[11], (DEPTH, N_BRANCHES, BRANCH_WIDTH, D_MODEL), F32) * (BRANCH_WIDTH ** -0.5)
    w_out = nrm(ks[12], (DEPTH, D_MODEL, D_MODEL), F32) * (D_MODEL ** -0.5)
    final_norm = 1.0 + 0.02 * nrm(ks[13], (D_MODEL,), F32)
    return {'x': x, 'norm_w': norm_w, 'w_in': w_in, 'conv_w': conv_w, 'a_log': a_log,
            'dt_bias': dt_bias, 'lb_param': lb_param, 'norm_a': norm_a, 'norm_b': norm_b,
            'sinks': sinks, 'rel_bias': rel_bias, 'w_branch': w_branch, 'w_out': w_out,
            'final_norm': final_norm}


def reference(x, norm_w, w_in, conv_w, a_log, dt_bias, lb_param, norm_a, norm_b,
              sinks, rel_bias, w_branch, w_out, final_norm):
    Bn, S, _ = x.shape
    split_points = np.cumsum(IN_WIDTHS)[:-1].tolist()
    lb_all = jnp.cumsum(jax.nn.softmax(lb_param.astype(F32), axis=0), axis=0)
    lb_all = lb_all - lb_all[0:1]
    bias_blk = _band_bias(rel_bias)
    for l in range(DEPTH):
        h = _rmsnorm(x, norm_w[l])
        proj = h @ w_in[l]
        (a_q, a_f, a_i, a_g, b_qkv, b_z, b_beta, b_a,
         c_q, c_k, c_v, c_g, gate_logits) = jnp.split(proj, split_points, axis=-1)
        y_a = _hgrn2_branch(a_q, a_f, a_i, a_g, lb_all[l], norm_a[l])
        y_b = _gated_deltanet_branch(b_qkv, b_z, b_beta, b_a, conv_w[l], a_log[l],
                                     dt_bias[l], norm_b[l])
        y_c = _swa_sink_branch(c_q, c_k, c_v, c_g, sinks[l], bias_blk)
        ys = jnp.stack([y_a, y_b, y_c], axis=2).astype(x.dtype)
        lifted = jnp.einsum('bsnc,ncd->bsnd', ys, w_branch[l])
        gates = jax.nn.sigmoid(gate_logits.reshape(Bn, S, N_BRANCHES, D_MODEL))
        merged = jnp.sum(gates * lifted, axis=2)
        x = x + merged @ w_out[l]
    return _rmsnorm(x, final_norm)
```

```python
import math
import os
from contextlib import ExitStack
import numpy as np
import concourse.bass as bass
import concourse.mybir as mybir
from concourse.bass_utils import run_bass_kernel_spmd

F32 = mybir.dt.float32
BF16 = mybir.dt.bfloat16
F32R = mybir.dt.float32r
AF = mybir.ActivationFunctionType
ALU = mybir.AluOpType
AX = mybir.AxisListType

D = 1024
DEPTH = 2
EPS = 1e-6
NW = 88128
T = 512
NEG = -30000.0

C_ID, C_ONES, C_OBLK, C_NOBLK, C_U, C_UM, C_NUM, C_R, C_MA, C_NEGS, C_SEL0, C_SEL1 = [i * 128 for i in range(12)]
C_MC = 12 * 128
C_OH = C_MC + 256
NCST = C_OH + 384
P_NW, P_CW, P_LBF, P_NA, P_NB, P_SK, P_AL, P_DT, P_RB, P_ROLE, P_NROLE, P_FC, P_LBT = 0, 24, 120, 128, 384, 640, 648, 656, 664, 672, 673, 674, 704
NPRM_S = 704
NPRM = 704 + 1024


class Tile:
    def __init__(self, h):
        self.h = h
        self.w = {}
        self.r = {}
        self.dsem = None
        self.dcnt = 0
        self.r32 = False

    def __getitem__(self, idx):
        return View(self, self.h[idx])

    def v(self, ap):
        return View(self, ap)


class View:
    def __init__(self, tile, ap):
        self.tile = tile
        self.ap = ap


def _o(v):
    if v.tile.r32 and v.ap.dtype == F32:
        return v.ap.bitcast(F32R)
    return v.ap


class Prog:
    def __init__(self, nc, es):
        self.nc = nc
        self.es = es
        self.eng = {}
        for name in ("pe", "act", "dve", "pool", "sp"):
            sem = es.enter_context(nc.semaphore("s_" + name))
            self.eng[name] = dict(sem=sem, cnt=0, ops=[], waited={})
        self.final_waits = []

    def _deps(self, eng, reads, writes):
        E = self.eng[eng]
        waits = {}

        def need(key, ent):
            sem, val, who = ent
            if who == eng and eng in ("pe", "sp"):
                return
            if E["waited"].get(key, 0) >= val:
                return
            if key not in waits or waits[key][1] < val:
                waits[key] = (sem, val)

        for v in reads:
            for k, ent in v.tile.w.items():
                need(k, ent)
        for v in writes:
            for k, ent in v.tile.w.items():
                need(k, ent)
            for k, ent in v.tile.r.items():
                need(k, ent)
        for k, (sem, val) in waits.items():
            E["waited"][k] = val
        return list(waits.values())

    def op(self, eng, fn, reads, writes):
        E = self.eng[eng]
        waits = self._deps(eng, reads, writes)
        E["cnt"] += 1
        idx = E["cnt"]
        E["ops"].append((waits, fn, (E["sem"], 1)))
        key = id(E["sem"])
        for v in reads:
            v.tile.r[key] = (E["sem"], idx, eng)
        for v in writes:
            v.tile.w[key] = (E["sem"], idx, eng)

    def dma_in(self, out_v, in_ap, src_tile=None):
        t = out_v.tile
        if t.dsem is None:
            t.dsem = self.es.enter_context(self.nc.semaphore())
        rd = [View(src_tile, None)] if src_tile is not None else []
        waits = self._deps("sp", rd, [out_v])
        t.dcnt += 16
        oap = out_v.ap
        self.eng["sp"]["ops"].append((waits, lambda e: e.dma_start(out=oap, in_=in_ap), (t.dsem, 16)))
        t.w[id(t.dsem)] = (t.dsem, t.dcnt, "dma")
        if src_tile is not None:
            src_tile.r[id(t.dsem)] = (t.dsem, t.dcnt, "dma")

    def collective(self, src_tile, dst_tile, fn):
        if getattr(self, "ccsem", None) is None:
            self.ccsem = self.es.enter_context(self.nc.semaphore("ccsem"))
            self.cccnt = 0
        waits = self._deps("pool", [View(src_tile, None)], [View(dst_tile, None)])
        self.cccnt += 1
        self.eng["pool"]["ops"].append((waits, fn, (self.ccsem, 1)))
        k = id(self.ccsem)
        src_tile.r[k] = (self.ccsem, self.cccnt, "cc")
        dst_tile.w[k] = (self.ccsem, self.cccnt, "cc")

    def dma_cast(self, out_v, in_ap, src_tile=None):
        t = out_v.tile
        if t.dsem is None:
            t.dsem = self.es.enter_context(self.nc.semaphore())
        rd = [View(src_tile, None)] if src_tile is not None else []
        waits = self._deps("pool", rd, [out_v])
        t.dcnt += 16
        oap = out_v.ap
        self.eng["pool"]["ops"].append((waits, lambda e: e.dma_start(out=oap, in_=in_ap), (t.dsem, 16)))
        t.w[id(t.dsem)] = (t.dsem, t.dcnt, "dma")

    def dma_out(self, out_ap, in_v, final=False, dst_tile=None):
        t = in_v.tile
        if t.dsem is None:
            t.dsem = self.es.enter_context(self.nc.semaphore())
        wr = [View(dst_tile, None)] if dst_tile is not None else []
        waits = self._deps("sp", [in_v], wr)
        if dst_tile is not None:
            dst_tile.w[id(t.dsem)] = (t.dsem, t.dcnt + 16, "dma")
        t.dcnt += 16
        iap = in_v.ap
        self.eng["sp"]["ops"].append((waits, lambda e: e.dma_start(out=out_ap, in_=iap), (t.dsem, 16)))
        t.r[id(t.dsem)] = (t.dsem, t.dcnt, "dma")
        if final:
            self.final_waits.append((t.dsem, t.dcnt))

    def replay(self, name, e):
        for waits, fn, inc in self.eng[name]["ops"]:
            for sem, val in waits:
                e.wait_ge(sem, val)
            ins = fn(e)
            ins.then_inc(inc[0], inc[1])
        if name == "sp":
            for sem, val in self.final_waits:
                e.wait_ge(sem, val)

    def mm(self, out, lhsT, rhs, start=True, stop=True):
        o, l, r = out.ap, lhsT.ap, rhs.ap
        self.op("pe", lambda e: e.matmul(o, lhsT=l, rhs=r, start=start, stop=stop), [lhsT, rhs], [out])

    def tr(self, out, in_, ident):
        o, i, d = out.ap, in_.ap, ident.ap
        self.op("pe", lambda e: e.transpose(out=o, in_=i, identity=d), [in_, ident], [out])

    def act(self, out, in_, func, scale=1.0, bias=0.0, eng="act"):
        o, i = _o(out), in_.ap
        rd = [in_]
        sc, bi = scale, bias
        if isinstance(scale, View):
            rd.append(scale)
            sc = scale.ap
        if isinstance(bias, View):
            rd.append(bias)
            bi = bias.ap
        self.op("act", lambda e: e.activation(out=o, in_=i, func=func, bias=bi, scale=sc), rd, [out])

    def tt(self, out, in0, in1, op, eng="dve"):
        o, a, b = _o(out), in0.ap, in1.ap
        self.op(eng, lambda e: e.tensor_tensor(out=o, in0=a, in1=b, op=op), [in0, in1], [out])

    def ts(self, out, in0, s1, op0, s2=None, op1=None, eng="dve"):
        o, a = _o(out), in0.ap
        rd = [in0]
        x1, x2 = s1, s2
        if isinstance(s1, View):
            rd.append(s1)
            x1 = s1.ap
        if isinstance(s2, View):
            rd.append(s2)
            x2 = s2.ap
        if op1 is None:
            self.op(eng, lambda e: e.tensor_scalar(out=o, in0=a, scalar1=x1, scalar2=None, op0=op0), rd, [out])
        else:
            self.op(eng, lambda e: e.tensor_scalar(out=o, in0=a, scalar1=x1, scalar2=x2, op0=op0, op1=op1), rd, [out])

    def stt(self, out, in0, scalar, op0, in1, op1):
        o, a, b = _o(out), in0.ap, in1.ap
        rd = [in0, in1]
        s = scalar
        if isinstance(scalar, View):
            rd.append(scalar)
            s = scalar.ap
        self.op("dve", lambda e: e.scalar_tensor_tensor(out=o, in0=a, scalar=s, in1=b, op0=op0, op1=op1), rd, [out])

    def cp(self, out, in_, eng="dve"):
        o, i = _o(out), in_.ap
        if eng == "act":
            self.op("act", lambda e: e.activation(out=o, in_=i, func=AF.Copy), [in_], [out])
        else:
            self.op(eng, lambda e: e.tensor_copy(out=o, in_=i), [in_], [out])

    def recip(self, out, in_):
        o, i = out.ap, in_.ap
        self.op("dve", lambda e: e.reciprocal(out=o, in_=i), [in_], [out])

    def cpred(self, out, mask, data):
        o, m, d = out.ap, mask.ap, data.ap
        self.op("dve", lambda e: e.copy_predicated(out=o, mask=m, data=d), [out, mask, data], [out])

    def memset(self, out, val, eng="pool"):
        o = out.ap
        self.op(eng, lambda e: e.memset(o, val), [], [out])

    def reduce_sum(self, out, in_):
        o, i = out.ap, in_.ap
        self.op("dve", lambda e: e.tensor_reduce(out=o, in_=i, axis=AX.X, op=ALU.add), [in_], [out])


def build(S):
    nc = bass.Bass("TRN2", target_bir_lowering=False)
    NST = S // T
    NIT = NST + 1
    xT_d = nc.dram_tensor("xT", [8, 128, S + T], F32, kind="ExternalInput").ap()
    w_d = nc.dram_tensor("wts", [128, NW], F32, kind="ExternalInput").ap()
    cst_d = nc.dram_tensor("cst", [128, NCST], F32, kind="ExternalInput").ap()
    prm_d = nc.dram_tensor("prm", [128, NPRM], F32, kind="ExternalInput").ap()
    out_d = nc.dram_tensor("outT", [8, 128, S + T], F32, kind="ExternalOutput").ap()
    wbf_d = nc.dram_tensor("wbf", [128, NW], BF16, kind="Internal").ap()
    src_d = nc.dram_tensor("xsrc", [1024, T], F32, kind="Internal").ap()
    dst_d = nc.dram_tensor("xdst", [2048, T], F32, kind="Internal").ap()
    tbs_h = nc.dram_tensor("tbs", [128, 8, 384], F32, kind="Internal")
    tbs_d = tbs_h.ap()

    with ExitStack() as es:
        P = Prog(nc, es)

        def sb(name, shape, dt=F32):
            return Tile(es.enter_context(nc.sbuf_tensor(name, shape, dt)))

        def ps(name, shape, dt=F32):
            return Tile(es.enter_context(nc.psum_tensor(name, shape, dt)))

        xT = sb("xT_s", [128, 8, T])
        hT = sb("hT", [128, 8, T], BF16)
        yT = sb("yT", [128, 12, T], BF16)
        mT = sb("mT", [128, 8, T], BF16)
        RA = sb("RA", [128, 16448], BF16)
        RB = sb("RB", [128, 17408], BF16)
        CST = sb("CST", [128, NCST])
        PRM = sb("PRM", [128, NPRM_S])
        identb = sb("identb", [128, 128], BF16)
        onesr = sb("onesr", [128, 128])
        onesr.r32 = True
        onesb = sb("onesb", [128, 128], BF16)
        EB = sb("EB", [128, 2, 8, 128], BF16)
        omlT = sb("omlT", [128, 2, 4])
        omlB = sb("omlB", [128, 1, 512])
        negeal = sb("negeal", [128, 2, 4])
        esink = sb("esink", [128, 2, 4])
        SA = sb("SA", [128, 1, 4, 128])
        SAb = sb("SAb", [128, 1, 4, 128], BF16)
        SBs = sb("SBs", [128, 1, 4, 128])
        SBb = sb("SBb", [128, 1, 4, 128], BF16)
        pc = sb("pc", [128, 1, 12, 131], BF16)
        kTc = sb("kTc", [128, 1, 2, 128], BF16)
        vC = sb("vC", [128, 1, 2, 128], BF16)
        dg = sb("dg", [128, 12, 4, 128], BF16)
        fA = [sb(f"fA{i}", [128, 512]) for i in range(10)]
        fA_role = sb("rolem", [128, 512])
        srcT, dstT = Tile(None), Tile(None)
        bA = [sb(f"bA{i}", [128, 512], BF16) for i in range(6)]
        sm = [sb(f"sm{i}", [128, 16]) for i in range(9)]
        fB = [sb(f"fB{i}", [128, 512]) for i in range(10)]
        for t_ in fB:
            t_.r32 = True
        bB = [sb(f"bB{i}", [128, 512], BF16) for i in range(6)]
        PB = [ps(f"pb{i}", [128, 512]) for i in range(8)]

        class Rot:
            def __init__(self, banks):
                self.banks, self.i = banks, 0

            def __call__(self):
                self.i += 1
                return self.banks[self.i % len(self.banks)]
        rotA, rotB = Rot([PB[2], PB[3]]), Rot([PB[5], PB[6], PB[7]])
        rotAll = Rot([PB[2], PB[3], PB[5], PB[6], PB[7]])
        pbank = rotAll
        L0, L1 = PB[0], PB[1]

        def Rv(v):
            return v.tile.v(v.ap.bitcast(F32R))

        def cs(off, n=128, rows=128):
            return CST[0:rows, off:off + n]

        ident = cs(C_ID)

        P.dma_in(CST[:, :], cst_d)
        P.dma_in(PRM[:, :], prm_d[:, 0:NPRM_S])
        P.dma_in(fA[7][:, :], prm_d[:, P_LBT:P_LBT + 512])
        P.dma_in(fA[8][:, :], prm_d[:, P_LBT + 512:P_LBT + 1024])
        P.cp(identb[:, :], cs(C_ID))
        P.cp(onesb[:, :], cs(C_ONES))
        P.cp(onesr[:, :], cs(C_ONES))
        for t_ in (SA, SBs):
            P.memset(t_[:, :, :, :], 0.0)
        for t_ in (SAb, SBb):
            P.memset(t_[:, :, :, :], 0.0)
        P.memset(pc[:, :, :, :], 0.0)
        P.memset(kTc[:, :, :, :], 0.0)
        P.memset(vC[:, :, :, :], 0.0)
        P.tt(fA[0][:, :], fA[8][:, :], fA[7][:, :], ALU.subtract)
        P.act(fA[1][:, :], fA[0][:, :], AF.Sigmoid)
        P.ts(omlB[:, 0, :], fA[1][:, :], PRM[:, P_NROLE:P_NROLE + 1], ALU.mult, 1.0, ALU.add)
        lbf = PRM[:, P_LBF:P_LBF + 8].ap.rearrange("p (l c) -> p l c", l=2)
        P.tt(sm[0][:, 0:4], PRM.v(lbf[:, 1, :]), PRM.v(lbf[:, 0, :]), ALU.subtract)
        P.act(sm[0][:, 4:8], sm[0][:, 0:4], AF.Sigmoid)
        P.ts(omlT[:, 0, :], sm[0][:, 4:8], PRM[:, P_NROLE:P_NROLE + 1], ALU.mult, 1.0, ALU.add)
        rolem = fA_role
        P.cp(rolem[:, :], PRM.v(PRM[:, P_ROLE:P_ROLE + 1].ap.broadcast_to([128, 512])))
        P.act(negeal[:, :, :], PRM.v(PRM[:, P_AL:P_AL + 8].ap.rearrange("p (l c) -> p l c", l=2)), AF.Exp)
        P.ts(negeal[:, :, :], negeal[:, :, :], -1.0, ALU.mult)
        P.act(esink[:, :, :], PRM.v(PRM[:, P_SK:P_SK + 8].ap.rearrange("p (l c) -> p l c", l=2)), AF.Exp)
        RAf = RA.v(RA[:, 0:6144].ap.bitcast(F32).rearrange("p (h c) -> p h c", h=8))
        RAg = RA.v(RA[:, 6144:12288].ap.bitcast(F32).rearrange("p (h c) -> p h c", h=8))
        for h in range(8):
            P.ts(RA.v(RAg.ap[0:32, h, :]), CST[0:32, C_OH:C_OH + 384], PRM[0:32, P_RB + h:P_RB + h + 1], ALU.mult)
        for h in range(8):
            pb = pbank()
            P.mm(pb[:, 0:384], CST[0:32, C_ONES:C_ONES + 128], RA.v(RAg.ap[0:32, h, :]))
            P.cp(RA.v(RAf.ap[:, h, :]), pb[:, 0:384], eng="act")
        P.dma_out(tbs_d, RAf)
        EBf = RB.v(RB[:, 0:4096].ap.bitcast(F32).rearrange("p (k h q) -> p k h q", k=2, h=8))
        for kb in range(2):
            src = bass.AP(tensor=tbs_d.tensor, offset=128 * (1 - kb) + 127, ap=[[3071, 128], [384, 8], [1, 128]])
            waits = [(RA.dsem, RA.dcnt)]
            t = RB
            if t.dsem is None:
                t.dsem = es.enter_context(nc.semaphore())
            t.dcnt += 16
            oap = EBf.ap[:, kb, :, :]
            P.eng["sp"]["ops"].append((waits, (lambda e, oap=oap, src=src: e.dma_start(out=oap, in_=src)), (t.dsem, 16)))
            t.w[id(t.dsem)] = (t.dsem, t.dcnt, "dma")
        P.act(EBf, EBf, AF.Exp)
        mc = CST[:, C_MC:C_MC + 256].ap.rearrange("p (k q) -> p k q", k=2)
        for kb in range(2):
            P.tt(EB[:, kb, :, :], RB.v(EBf.ap[:, kb, :, :]), CST.v(mc[:, kb:kb + 1, :].broadcast_to([128, 8, 128])), ALU.mult)

        for ch in range(12):
            for j in range(4):
                c0 = P_CW + ch * 4 + j
                P.ts(dg[:, ch, j, :], cs(C_ID), PRM[:, c0:c0 + 1], ALU.mult)

        wbfT = Tile(None)
        use_f32 = [True]

        def convert_weights():
            CH = 4096
            for off in range(0, NW, CH):
                n_ = min(CH, NW - off)
                P.dma_cast(View(wbfT, wbf_d[:, off:off + n_]), w_d[:, off:off + n_])

        def load(l, src_off, n, dst_view, hw=False):
            if use_f32[0]:
                P.dma_cast(dst_view, w_d[:, src_off:src_off + n])
            elif hw:
                P.dma_in(dst_view, wbf_d[:, src_off:src_off + n], src_tile=wbfT)
            else:
                P.dma_cast(dst_view, wbf_d[:, src_off:src_off + n], src_tile=wbfT)

        OFF_A, OFF_B, OFF_C, OFF_D1, OFF_D2 = 0, 16384, 16384 + 16448, 16384 + 16448 + 10240, 16384 + 16448 + 10240 + 36864

        WA = RA.v(RA[:, 0:16384].ap.rearrange("p (h k c) -> p h k c", h=4, k=8))
        WBq = RB.v(RB[:, 0:12288].ap.rearrange("p (ch k c) -> p ch k c", ch=12, k=8))
        WBz = RB.v(RB[:, 12288:16384].ap.rearrange("p (k c) -> p k c", k=8))
        WBba = RB.v(RB[:, 16384:16448].ap.rearrange("p (k c) -> p k c", k=8))
        WCq = RA.v(RA[:, 0:4096].ap.rearrange("p (j k c) -> p j k c", j=4, k=8))
        WCk = RA.v(RA[:, 4096:5120].ap.rearrange("p (k c) -> p k c", k=8))
        WCv = RA.v(RA[:, 5120:6144].ap.rearrange("p (k c) -> p k c", k=8))
        WCg = RA.v(RA[:, 6144:10240].ap.rearrange("p (j k c) -> p j k c", j=4, k=8))

        def D1slot(slot):
            return (RB, slot * 4608) if slot < 2 else (RA, 10240)

        def WD1(slot):
            R_, base = D1slot(slot)
            g = R_.v(R_[:, base:base + 3072].ap.rearrange("p (n k c) -> p n k c", n=3, k=8))
            b = R_.v(R_[:, base + 3072:base + 4608].ap.rearrange("p (k c) -> p k c", k=12))
            return g, b
        WO = RB.v(RB[:, 9216:17408].ap.rearrange("p (o k c) -> p o k c", o=8, k=8))

        def load_A(l):
            for h in range(4):
                load(l, OFF_A + h * 4096, 4096, RA[:, h * 4096:(h + 1) * 4096])

        def load_B(l):
            for u in range(4):
                load(l, OFF_B + u * 4096, 4096, RB[:, u * 4096:(u + 1) * 4096])
            load(l, OFF_B + 16384, 64, RB[:, 16384:16448])

        def load_C(l):
            load(l, OFF_C, 4096, RA[:, 0:4096])
            load(l, OFF_C + 4096, 4096, RA[:, 4096:8192])
            load(l, OFF_C + 8192, 2048, RA[:, 8192:10240])

        def load_D1(l, dc):
            R_, base = D1slot(dc % 3)
            load(l, OFF_D1 + dc * 4608, 4608, R_[:, base:base + 4608], hw=True)

        def load_D2(l):
            load(l, OFF_D2, 4096, RB[:, 9216:9216 + 4096], hw=True)
            load(l, OFF_D2 + 4096, 4096, RB[:, 9216 + 4096:17408], hw=True)

        def r3(v):
            return v.tile.v(v.ap.rearrange("p (h c) -> p h c", h=4))

        def b3(tl, c0):
            return tl.v(tl[:, c0:c0 + 4].ap.unsqueeze(2).broadcast_to([128, 4, 128]))

        def cb(off):
            return CST.v(CST[:, off:off + 128].ap.unsqueeze(1).broadcast_to([128, 4, 128]))

        def rms_rstd():
            pss = pbank()
            for kc in range(8):
                sq = fB[kc % 2]
                P.act(sq[:, :], xT[:, kc, :], AF.Square)
                P.mm(pss[:, :], Rv(onesr[:, :]), Rv(sq[:, :]), start=(kc == 0), stop=(kc == 7))
            P.act(fA[2][:, :], pss[:, :], AF.Ln, scale=1.0 / D, bias=EPS)
            P.act(fA[3][:, :], fA[2][:, :], AF.Exp, scale=-0.5)
            return fA[3]

        def rmsnorm(wl, dst_fn):
            rs = rms_rstd()
            for kc in range(8):
                P.stt(dst_fn(kc), xT[:, kc, :], PRM[:, P_NW + wl * 8 + kc:P_NW + wl * 8 + kc + 1], ALU.mult,
                      rs[:, :], ALU.mult)

        def gated_store(o_v, gate_ps, nwb_v, ychunk0, tk, F, Bf, smt, rot):
            osq, t1, gs = F[4], F[5], F[6]
            P.act(gs[:, :], gate_ps, AF.Silu)
            P.act(osq[:, :], o_v, AF.Square)
            P.reduce_sum(smt[:, 0:4], r3(osq[:, :]))
            P.act(smt[:, 4:8], smt[:, 0:4], AF.Ln, scale=1.0 / 128, bias=EPS)
            P.act(smt[:, 8:12], smt[:, 4:8], AF.Exp, scale=-0.5)
            P.tt(r3(t1[:, :]), r3(o_v), b3(smt, 8), ALU.mult)
            P.tt(r3(gs[:, :]), r3(gs[:, :]), nwb_v.tile.v(nwb_v.ap.unsqueeze(1).broadcast_to([128, 4, 128])), ALU.mult)
            yb = Bf[5]
            P.tt(yb[:, :], t1[:, :], gs[:, :], ALU.mult)
            pbt = rot()
            pbv = pbt.v(pbt[:, 0:256].ap.bitcast(BF16))
            for h in range(4):
                P.tr(pbt.v(pbv.ap[:, h * 128:(h + 1) * 128]), yb[:, h * 128:(h + 1) * 128], identb[:, :])
            P.cp(yT[:, ychunk0:ychunk0 + 4, tk * 128:(tk + 1) * 128], r3(pbv), eng="act")

        def branch_A(l):
            rot = rotA
            for tk in range(T // 128):
                tok = slice(tk * 128, (tk + 1) * 128)
                pg, po = L1, L0
                qTs, sgT, sig, negk, logf, e4 = fA[0], fA[1], fA[2], fA[3], fA[7], fA[8]
                ke, vb = bA[0], bA[1]

                def proj_fm(dst, c0):
                    for h in range(4):
                        hs = slice(h * 128, (h + 1) * 128)
                        for kc in range(8):
                            P.mm(dst[:, hs], RA.v(WA.ap[:, h, kc, c0:c0 + 128]), hT[:, kc, tok], start=(kc == 0), stop=(kc == 7))

                def proj_tm(dst, c0):
                    for h in range(4):
                        hs = slice(h * 128, (h + 1) * 128)
                        for kc in range(8):
                            P.mm(dst[:, hs], hT[:, kc, tok], RA.v(WA.ap[:, h, kc, c0:c0 + 128]), start=(kc == 0), stop=(kc == 7))
                ptm0 = rot()
                proj_tm(ptm0, 128)
                P.act(sig[:, :], ptm0[:, :], AF.Sigmoid)
                yield
                pf = rot()
                proj_fm(pf, 128)
                P.act(sgT[:, :], pf[:, :], AF.Sigmoid, scale=-1.0)
                P.stt(negk[:, :], sig[:, :], -1.0, ALU.add, omlB[:, l, :], ALU.mult)
                yield
                pq = rot()
                proj_fm(pq, 0)
                P.act(qTs[:, :], pq[:, :], AF.Silu)
                P.act(logf[:, :], negk[:, :], AF.Ln, scale=1.0, bias=1.0)
                yield
                ptm1 = rot()
                proj_tm(ptm1, 256)
                P.cp(vb[:, :], ptm1[:, :], eng="act")
                yield
                prev = rot()
                P.mm(prev[:, :], cs(C_R), logf[:, :])
                P.act(e4[:, :], prev[:, :], AF.Exp)
                P.stt(ke[:, :], negk[:, :], -1.0, ALU.mult, e4[:, :], ALU.mult)
                yield
                E1, E2, E3, tmp = fA[9], fA[4], fA[5], fA[6]
                qsT, qhT, khT, scT = bA[2], bA[3], bA[4], bA[5]
                for (Ex, coff) in ((E2, C_UM), (E3, C_NUM), (E1, C_U)):
                    pcx = rot()
                    for h in range(4):
                        hs = slice(h * 128, (h + 1) * 128)
                        P.mm(pcx[:, hs], logf[:, hs], cs(coff))
                    P.act(Ex[:, :], pcx[:, :], AF.Exp)
                    yield
                proj_tm(pg, 384)
                P.tt(qhT[:, :], qTs[:, :], E2[:, :], ALU.mult)
                P.tt(tmp[:, :], sgT[:, :], E3[:, :], ALU.mult)
                P.tt(r3(khT[:, :]), r3(tmp[:, :]), omlT.v(omlT[:, l, :].ap.unsqueeze(2).broadcast_to([128, 4, 128])), ALU.mult)
                P.tt(qsT[:, :], qTs[:, :], E1[:, :], ALU.mult)
                yield
                psc = rot()
                for h in range(4):
                    hs = slice(h * 128, (h + 1) * 128)
                    P.mm(psc[:, hs], khT[:, hs], qhT[:, hs])
                P.tt(r3(scT[:, :]), r3(psc[:, :]), cb(C_MA), ALU.mult)
                yield
                E13 = E1.v(E1[:, :].ap.rearrange("p (h c) -> p h c", h=4))
                for c in range(2):
                    r = slice(c * 64, (c + 1) * 64)
                    for h in range(4):
                        hs = slice(h * 128, (h + 1) * 128)
                        cc = slice(h * 128 + c * 64, h * 128 + (c + 1) * 64)
                        P.mm(po[r, hs], scT[:, cc], vb[:, hs], start=True, stop=False)
                        P.mm(po[r, hs], qsT[:, cc], SAb[:, l, h, :], start=False, stop=True)
                    pst = rot()
                    for h in range(4):
                        hs = slice(h * 128, (h + 1) * 128)
                        P.mm(pst[:, hs], ke[r, hs], vb[r, hs])
                    ebb = E1.v(E13.ap[:, :, c * 64 + 63:c * 64 + 64].broadcast_to([128, 4, 128]))
                    P.tt(SA[:, l, :, :], SA[:, l, :, :], ebb, ALU.mult)
                    P.tt(SA[:, l, :, :], SA[:, l, :, :], r3(pst[:, :]), ALU.add)
                    P.cp(SAb[:, l, :, :], SA[:, l, :, :], eng="act")
                    yield
                gated_store(po[:, :], pg[:, :], PRM[:, P_NA + l * 128:P_NA + (l + 1) * 128], 0, tk, fA, bA, sm[1], rot)
                yield

        def branch_B(l):
            rot = rotB
            F, Bf = fB, bB
            for tk in range(T // 128):
                tok = slice(tk * 128, (tk + 1) * 128)
                P.cp(pc[:, l, :, 0:3], pc[:, l, :, 128:131])
                for g3 in range(3):
                    pp = rot()
                    for j4 in range(4):
                        ch = g3 * 4 + j4
                        for kc in range(8):
                            P.mm(pp[:, j4 * 128:(j4 + 1) * 128], RB.v(WBq.ap[:, ch, kc, :]), hT[:, kc, tok], start=(kc == 0), stop=(kc == 7))
                    P.cp(pc[:, l, g3 * 4:(g3 + 1) * 4, 3:131], r3(pp[:, :]), eng="act")
                    yield
                pba, pz = rot(), PB[4]
                for kc in range(8):
                    P.mm(pba[:, 0:8], hT[:, kc, tok], RB.v(WBba.ap[:, kc, :]), start=(kc == 0), stop=(kc == 7))
                g_, beta, eg, ekd, ge01, bg = sm[2], sm[3], sm[4], sm[5], sm[6], sm[7]
                P.act(beta[:, 0:4], pba[:, 0:4], AF.Sigmoid)
                P.tt(g_[:, 4:8], pba[:, 4:8], PRM[:, P_DT + l * 4:P_DT + l * 4 + 4], ALU.add)
                for kc in range(8):
                    P.mm(pz[:, :], hT[:, kc, tok], RB.v(WBz.ap[:, kc, :]), start=(kc == 0), stop=(kc == 7))
                yield
                qs, ks, vT_, sq = F[0], F[1], F[2], F[3]
                for g3, dst in enumerate((qs, ks, vT_)):
                    pp = rot()
                    for j4 in range(4):
                        ch = g3 * 4 + j4
                        for j in range(4):
                            P.mm(pp[:, j4 * 128:(j4 + 1) * 128], dg[:, ch, j, :], pc[:, l, ch, j:j + 128], start=(j == 0), stop=(j == 3))
                    P.act(dst[:, :], pp[:, :], AF.Silu)
                    yield
                P.act(g_[:, 8:12], g_[:, 4:8], AF.Exp)
                P.act(g_[:, 12:16], g_[:, 8:12], AF.Ln, scale=1.0, bias=1.0)
                P.tt(g_[:, 0:4], g_[:, 12:16], negeal[:, l, :], ALU.mult)
                pgc = rot()
                P.mm(pgc[:, 0:4], cs(C_U), g_[:, 0:4])
                P.mm(pgc[:, 4:8], cs(C_OBLK), g_[:, 0:4])
                P.mm(pgc[:, 8:12], cs(C_SEL0), g_[:, 0:4])
                P.mm(pgc[:, 12:16], cs(C_SEL1), g_[:, 0:4])
                P.act(eg[:, 0:4], pgc[:, 0:4], AF.Exp)
                P.act(ge01[:, 0:8], pgc[:, 8:16], AF.Exp)
                P.cp(ekd[:, 4:12], pgc[:, 0:8], eng="act")
                P.tt(ekd[:, 12:16], ekd[:, 8:12], ekd[:, 4:8], ALU.subtract)
                P.act(ekd[:, 0:4], ekd[:, 12:16], AF.Exp)
                P.tt(bg[:, 0:4], beta[:, 0:4], eg[:, 0:4], ALU.mult)
                yield
                qn, kn, qnb = F[4], F[5], Bf[0]
                for (src, dstn, scl) in ((qs, qn, 128.0 ** -0.5), (ks, kn, 1.0)):
                    P.act(sq[:, :], src[:, :], AF.Square)
                    pn = rot()
                    P.mm(pn[:, :], Rv(onesr[:, :]), Rv(sq[:, :]))
                    P.act(sq[:, :], pn[:, :], AF.Ln, scale=1.0, bias=EPS)
                    P.act(sq[:, :], sq[:, :], AF.Exp, scale=-0.5)
                    P.stt(dstn[:, :], src[:, :], scl, ALU.mult, sq[:, :], ALU.mult)
                    yield
                P.cp(qnb[:, :], qn[:, :], eng="act")
                ptk, ptv = rot(), rot()
                for h in range(4):
                    hs = slice(h * 128, (h + 1) * 128)
                    P.tr(ptk[:, hs], kn[:, hs], ident)
                    P.tr(ptv[:, hs], vT_[:, hs], ident)
                vbt, kbg, kd = F[6], F[7], Bf[1]
                P.tt(r3(vbt[:, :]), r3(ptv[:, :]), b3(beta, 0), ALU.mult)
                P.tt(r3(kbg[:, :]), r3(ptk[:, :]), b3(bg, 0), ALU.mult)
                P.tt(r3(kd[:, :]), r3(ptk[:, :]), b3(ekd, 0), ALU.mult)
                yield
                gU, Em = F[8], F[9]
                P.tt(r3(gU[:, :]), cb(C_U), b3(g_, 0), ALU.mult)
                pD, pKK, pQK = rot(), rot(), rot()
                for h in range(4):
                    hs = slice(h * 128, (h + 1) * 128)
                    P.mm(pD[:, hs], gU[:, hs], cs(C_OBLK), start=True, stop=False)
                    P.mm(pD[:, hs], cs(C_NOBLK), gU[:, hs], start=False, stop=False)
                    P.mm(pD[:, hs], ident, cs(C_NEGS), start=False, stop=True)
                for h in range(4):
                    hs = slice(h * 128, (h + 1) * 128)
                    P.mm(pKK[:, hs], Rv(kn[:, hs]), Rv(kn[:, hs]))
                    P.mm(pQK[:, hs], Rv(qn[:, hs]), Rv(kn[:, hs]))
                P.act(Em[:, :], pD[:, :], AF.Exp)
                yield
                Xs, Ys, W = [F[0], F[1]], [F[2], F[3]], F[8]
                P.tt(Xs[0][:, :], pKK[:, :], Em[:, :], ALU.mult)
                P.tt(r3(Xs[0][:, :]), r3(Xs[0][:, :]), b3(beta, 0), ALU.mult)
                P.tt(r3(Em[:, :]), r3(Em[:, :]), cb(C_ID), ALU.add)
                P.tt(Em[:, :], pQK[:, :], Em[:, :], ALU.mult)
                yield
                pt1, pt2 = rot(), rot()
                for h in range(4):
                    hs = slice(h * 128, (h + 1) * 128)
                    P.tr(pt1[:, hs], Xs[0][:, hs], ident)
                    P.tr(pt2[:, hs], Em[:, hs], ident)
                qkT = Bf[2]
                P.cp(Ys[0][:, :], pt1[:, :], eng="act")
                P.cp(qkT[:, :], pt2[:, :], eng="act")
                P.tt(r3(W[:, :]), cb(C_ID), r3(Ys[0][:, :]), ALU.subtract)
                yield
                xi, yi = 0, 0
                for k in range(1, 6):
                    pX = rot()
                    for h in range(4):
                        hs = slice(h * 128, (h + 1) * 128)
                        P.mm(pX[:, hs], Rv(Ys[yi][:, hs]), Rv(Xs[xi][:, hs]))
                    if k < 5:
                        pY = rot()
                        for h in range(4):
                            hs = slice(h * 128, (h + 1) * 128)
                            P.mm(pY[:, hs], Rv(Xs[xi][:, hs]), Rv(Ys[yi][:, hs]))
                    xi = 1 - xi
                    P.cp(Xs[xi][:, :], pX[:, :], eng="act")
                    if k < 5:
                        yi = 1 - yi
                        P.cp(Ys[yi][:, :], pY[:, :], eng="dve")
                    yield
                    pW = rot()
                    for h in range(4):
                        hs = slice(h * 128, (h + 1) * 128)
                        P.mm(pW[:, hs], Rv(Xs[xi][:, hs]), Rv(W[:, hs]))
                    P.tt(W[:, :], W[:, :], pW[:, :], ALU.add)
                    yield
                pu, pwT = rot(), rot()
                for h in range(4):
                    hs = slice(h * 128, (h + 1) * 128)
                    P.mm(pu[:, hs], Rv(W[:, hs]), Rv(vbt[:, hs]))
                    P.mm(pwT[:, hs], Rv(kbg[:, hs]), Rv(W[:, hs]))
                u_, wT_, vnew, tmpo, otok = F[4], Bf[3], Bf[4], F[5], F[7]
                P.cp(u_[:, :], pu[:, :], eng="act")
                P.cp(wT_[:, :], pwT[:, :], eng="dve")
                yield
                for c in range(2):
                    r = slice(c * 64, (c + 1) * 64)
                    pa1 = rot()
                    for h in range(4):
                        hs = slice(h * 128, (h + 1) * 128)
                        cc = slice(h * 128 + c * 64, h * 128 + (c + 1) * 64)
                        P.mm(pa1[r, hs], wT_[:, cc], SBb[:, l, h, :])
                    P.tt(vnew[r, :], u_[r, :], pa1[r, :], ALU.subtract)
                    yield
                    pa2, pa3, pst = rot(), rot(), rot()
                    for h in range(4):
                        hs = slice(h * 128, (h + 1) * 128)
                        cc = slice(h * 128 + c * 64, h * 128 + (c + 1) * 64)
                        P.mm(pa2[r, hs], qnb[:, cc], SBb[:, l, h, :])
                        P.mm(pa3[r, hs], qkT[r, cc], vnew[r, hs])
                        P.mm(pst[:, hs], kd[r, hs], vnew[r, hs])
                    egb = eg.v(eg[r, 0:4].ap.unsqueeze(2).broadcast_to([64, 4, 128]))
                    P.tt(r3(tmpo[r, :]), r3(pa2[r, :]), egb, ALU.mult)
                    P.tt(otok[r, :], tmpo[r, :], pa3[r, :], ALU.add)
                    P.tt(SBs[:, l, :, :], SBs[:, l, :, :], b3(ge01, c * 4), ALU.mult)
                    P.tt(SBs[:, l, :, :], SBs[:, l, :, :], r3(pst[:, :]), ALU.add)
                    P.cp(SBb[:, l, :, :], SBs[:, l, :, :], eng="act")
                    yield
                gated_store(otok[:, :], pz[:, :], PRM[:, P_NB + l * 128:P_NB + (l + 1) * 128], 4, tk, F, Bf, sm[8], rot)
                yield

        def branch_C(l, st):
            rot = rotA
            for tk in range(T // 128):
                tok = slice(tk * 128, (tk + 1) * 128)
                blk = st * (T // 128) + tk
                cur, prv = blk % 2, (blk + 1) % 2
                qTc, gsC = bA[0], fA[0]
                pq = rot()
                for j in range(4):
                    for kc in range(8):
                        P.mm(pq[:, j * 128:(j + 1) * 128], RA.v(WCq.ap[:, j, kc, :]), hT[:, kc, tok], start=(kc == 0), stop=(kc == 7))
                P.cp(qTc[:, :], pq[:, :], eng="act")
                yield
                pkv = rot()
                for kc in range(8):
                    P.mm(pkv[:, 0:128], RA.v(WCk.ap[:, kc, :]), hT[:, kc, tok], start=(kc == 0), stop=(kc == 7))
                for kc in range(8):
                    P.mm(pkv[:, 128:256], hT[:, kc, tok], RA.v(WCv.ap[:, kc, :]), start=(kc == 0), stop=(kc == 7))
                P.cp(kTc[:, l, cur, :], pkv[:, 0:128], eng="act")
                P.cp(vC[:, l, cur, :], pkv[:, 128:256], eng="act")
                yield
                pN, pDn = L0, L1
                for g in range(2):
                    gp = slice(g * 64, (g + 1) * 64)
                    kbs = [(prv, 0), (cur, 1)]
                    for i_kb, (buf, kbi) in enumerate(kbs):
                        psx = rot()
                        P.mm(psx[:, :], kTc[gp, l, buf, :], qTc[gp, :])
                        pe_, pm = fA[1 + (g * 2 + i_kb) % 2], bA[1 + (g * 2 + i_kb) % 2]
                        P.act(pe_[:, :], psx[:, :], AF.Exp, scale=0.125)
                        P.tt(r3(pm[:, :]), r3(pe_[:, :]), EB[:, kbi, g * 4:(g + 1) * 4, :], ALU.mult)
                        if tk == 0 and kbi == 0:
                            P.ts(pm[:, :], pm[:, :], PRM[:, P_FC + st:P_FC + st + 1], ALU.mult)
                        P.mm(pN[gp, :], vC[:, l, buf, g * 64:(g + 1) * 64], pm[:, :], start=(i_kb == 0), stop=(i_kb == len(kbs) - 1))
                        P.mm(pDn[gp, :], onesb[:, 0:64], pm[:, :], start=(i_kb == 0), stop=(i_kb == len(kbs) - 1))
                        yield
                pg = rot()
                for j in range(4):
                    for kc in range(8):
                        P.mm(pg[:, j * 128:(j + 1) * 128], RA.v(WCg.ap[:, j, kc, :]), hT[:, kc, tok], start=(kc == 0), stop=(kc == 7))
                P.act(gsC[:, :], pg[:, :], AF.Silu)
                den, o_ = fA[3], fA[4]
                P.tt(r3(den[:, :]), r3(pDn[:, :]), esink.v(esink[:, l, :].ap.unsqueeze(2).broadcast_to([128, 4, 128])), ALU.add)
                P.act(den[:, :], den[:, :], AF.Ln)
                P.act(den[:, :], den[:, :], AF.Exp, scale=-1.0)
                P.tt(o_[:, :], pN[:, :], den[:, :], ALU.mult)
                P.tt(yT[:, 8:12, tok], r3(o_[:, :]), r3(gsC[:, :]), ALU.mult)
                yield

        def phase_D1(l, dc):
            Wg, Wb = WD1(dc % 3)
            WT = RB if dc % 3 < 2 else RA
            macc, sg, t2 = fA[0], fA[1], fA[2]
            for n in range(3):
                pg, pl = pbank(), pbank()
                for kc in range(8):
                    P.mm(pg[:, :], WT.v(Wg.ap[:, n, kc, :]), hT[:, kc, :], start=(kc == 0), stop=(kc == 7))
                for c in range(4):
                    P.mm(pl[:, :], WT.v(Wb.ap[:, n * 4 + c, :]), yT[:, n * 4 + c, :], start=(c == 0), stop=(c == 3))
                P.act(sg[:, :], pg[:, :], AF.Sigmoid)
                if n == 0:
                    P.tt(macc[:, :], sg[:, :], pl[:, :], ALU.mult)
                elif n == 1:
                    P.tt(t2[:, :], sg[:, :], pl[:, :], ALU.mult)
                    P.tt(macc[:, :], macc[:, :], t2[:, :], ALU.add)
                else:
                    P.tt(t2[:, :], sg[:, :], pl[:, :], ALU.mult)
                    P.tt(mT[:, dc, :], macc[:, :], t2[:, :], ALU.add)

        def phase_D2(l):
            for dp in range(8):
                po = pbank()
                for dc in range(8):
                    P.mm(po[:, :], RB.v(WO.ap[:, dp, dc, :]), mT[:, dc, :], start=(dc == 0), stop=(dc == 7))
                P.tt(xT[:, dp, :], xT[:, dp, :], po[:, :], ALU.add)

        rg = [[0, 1], [2, 3], [4, 5], [6, 7]]
        outb = [fA[8], fA[9]]
        l = 0
        P.memset(fA[0][:, :], 0.0, eng="dve")
        for kc in range(8):
            P.dma_out(src_d[kc * 128:(kc + 1) * 128, :], fA[0][:, :], dst_tile=srcT)
        load_A(0)
        for it in range(NIT):
            P.collective(srcT, dstT, lambda e: e.collective_compute("AllGather", ALU.bypass, replica_groups=rg,
                                                                   ins=[src_d], outs=[dst_d]))
            P.dma_in(xT[:, :, :], xT_d[:, :, it * T:(it + 1) * T].rearrange("k p t -> p k t"))
            for kc in range(8):
                G = fB[2 + kc]
                P.dma_in(G[:, :], dst_d[kc * 128:(kc + 1) * 128, :], src_tile=dstT)
                P.cpred(xT[:, kc, :], rolem.v(rolem[:, :].ap.bitcast(mybir.dt.uint32)), G[:, :])
            rmsnorm(l, lambda kc: hT[:, kc, :])
            load_B(l)
            if it == 0:
                convert_weights()

            def streamA():
                yield from branch_A(l)
                load_C(l)
                yield from branch_C(l, it)
            gA, gB = streamA(), branch_B(l)
            doneA = doneB = False
            nb = 0
            RA_STEPS = int(os.environ.get("K_RA", "1"))
            RB_STEPS = int(os.environ.get("K_RB", "1"))
            while not (doneA and doneB):
                for _ in range(RA_STEPS):
                    if not doneA:
                        try:
                            next(gA)
                        except StopIteration:
                            doneA = True
                for _ in range(RB_STEPS):
                    if not doneB:
                        try:
                            next(gB)
                        except StopIteration:
                            doneB = True
                            load_D1(l, 0)
                            load_D1(l, 1)
            load_D1(l, 2)
            load_D2(l)
            for dc in range(8):
                phase_D1(l, dc)
                if dc + 3 < 8:
                    load_D1(l, dc + 3)
                if dc == 5 and it + 1 < NIT:
                    use_f32[0] = False
                    load_A(l)
            phase_D2(l)
            for kc in range(8):
                P.dma_out(src_d[kc * 128:(kc + 1) * 128, :], xT[:, kc, :], dst_tile=srcT)
            rs = rms_rstd()
            for kc in range(8):
                ob = outb[kc % 2]
                P.stt(ob[:, :], xT[:, kc, :], PRM[:, P_NW + 16 + kc:P_NW + 16 + kc + 1], ALU.mult, rs[:, :], ALU.mult)
                P.dma_out(out_d[kc, :, it * T:(it + 1) * T], ob[:, :], final=True)

        with nc.Block() as block:
            @block.tensor
            def _(e):
                P.replay("pe", e)

            @block.scalar
            def _(e):
                P.replay("act", e)

            @block.vector
            def _(e):
                P.replay("dve", e)

            @block.gpsimd
            def _(e):
                P.replay("pool", e)

            @block.sync
            def _(e):
                P.replay("sp", e)
    return nc


def _t5_bucket_np(dist):
    dist = np.asarray(dist)
    d_f = np.maximum(dist, 1).astype(np.float32)
    large = 16 + (np.log(d_f / np.float32(16)) / np.float32(math.log(128 / 16)) * np.float32(16)).astype(np.int32)
    large = np.minimum(large, 31)
    return np.where(dist < 16, dist, large)


def _consts():
    c = np.zeros((128, NCST), np.float32)
    r = np.arange(128)[:, None]
    t = np.arange(128)[None, :]
    same = (r // 64) == (t // 64)
    c[:, C_ID:C_ID + 128] = np.eye(128)
    c[:, C_ONES:C_ONES + 128] = 1.0
    c[:, C_OBLK:C_OBLK + 128] = same
    c[:, C_NOBLK:C_NOBLK + 128] = -1.0 * same
    U = same & (r <= t)
    Um = same * ((r <= t).astype(np.float32) - ((r % 64) <= 31).astype(np.float32))
    c[:, C_U:C_U + 128] = U
    c[:, C_UM:C_UM + 128] = Um
    c[:, C_NUM:C_NUM + 128] = -Um
    c[:, C_R:C_R + 128] = same & (r > t)
    c[:, C_MA:C_MA + 128] = same & (r <= t)
    c[:, C_NEGS:C_NEGS + 128] = np.where(same & (t < r), 0.0, NEG)
    c[:, C_SEL0:C_SEL0 + 128] = (r // 64 == 0) * np.ones((1, 128))
    c[:, C_SEL1:C_SEL1 + 128] = (r // 64 == 1) * np.ones((1, 128))
    j = np.arange(128)[:, None, None]
    kb = np.arange(2)[None, :, None]
    q = np.arange(128)[None, None, :]
    dist = q + 128 - (kb * 128 + j)
    c[:, C_MC:C_MC + 256] = ((dist >= 0) & (dist < 128)).reshape(128, 256)
    dp = np.arange(384)
    dd = dp - 127
    valid = (dd >= 0) & (dd < 128)
    b = _t5_bucket_np(np.maximum(dd, 0))
    oh = np.zeros((32, 384), np.float32)
    oh[b[valid], dp[valid]] = 1.0
    c[0:32, C_OH:C_OH + 384] = oh
    return c


def _pk(mat):
    n = mat.shape[1]
    return np.ascontiguousarray(mat.reshape(8, 128, n).transpose(1, 0, 2)).reshape(128, 8 * n)


def _pack_weights(w_in, w_branch, w_out):
    out = np.zeros((DEPTH, 128, NW), np.float32)
    for l in range(DEPTH):
        W = w_in[l]
        u = []
        for h in range(4):
            u.append(_pk(np.concatenate([W[:, part * 512 + h * 128: part * 512 + (h + 1) * 128] for part in range(4)], axis=1)))
        for ch in range(12):
            u.append(_pk(W[:, 2048 + ch * 128: 2048 + (ch + 1) * 128]))
        u.append(_pk(W[:, 3584:4096]))
        u.append(_pk(W[:, 4096:4104]))
        pair = lambda base, j: np.concatenate([np.arange(base + j * 64, base + (j + 1) * 64),
                                               np.arange(base + (4 + j) * 64, base + (5 + j) * 64)])
        for j in range(4):
            u.append(_pk(W[:, pair(4104, j)]))
        u.append(_pk(W[:, 4616:4744]))
        u.append(_pk(W[:, 4744:4872]))
        for j in range(4):
            u.append(_pk(W[:, pair(4872, j)]))
        for dc in range(8):
            for n in range(3):
                u.append(_pk(W[:, 5384 + n * 1024 + dc * 128: 5384 + n * 1024 + (dc + 1) * 128]))
            blocks = []
            for n in range(3):
                for c in range(4):
                    if n < 2:
                        rows = np.arange(c * 128, (c + 1) * 128)
                    else:
                        rows = pair(0, c)
                    blocks.append(w_branch[l, n][rows, dc * 128:(dc + 1) * 128])
            u.append(np.concatenate(blocks, axis=1))
        for dp in range(8):
            u.append(_pk(w_out[l][:, dp * 128:(dp + 1) * 128]))
        cat = np.concatenate(u, axis=1)
        assert cat.shape == (128, NW), cat.shape
        out[l] = cat
    return out


def _params(norm_w, conv_w, a_log, dt_bias, lb_param, norm_a, norm_b, sinks, rel_bias, final_norm):
    p = np.zeros((128, NPRM), np.float32)
    nw = np.stack([norm_w[0], norm_w[1], final_norm])
    p[:, P_NW:P_NW + 24] = nw.reshape(3, 8, 128).transpose(2, 0, 1).reshape(128, 24)
    cw = conv_w.reshape(2, 4, 12, 128).transpose(3, 0, 2, 1)
    p[:, P_CW:P_CW + 96] = cw.reshape(128, 96)
    p[:, P_LBT:P_LBT + 1024] = np.broadcast_to(lb_param.reshape(1, 1024), (128, 1024))
    p[:, P_LBF:P_LBF + 8] = lb_param.reshape(2, 4, 128).transpose(2, 0, 1).reshape(128, 8)
    p[:, P_NA:P_NA + 256] = np.broadcast_to(norm_a.reshape(1, 256), (128, 256))
    p[:, P_NB:P_NB + 256] = np.broadcast_to(norm_b.reshape(1, 256), (128, 256))
    sk = np.zeros((128, 2, 4), np.float32)
    for l in range(2):
        for g in range(2):
            sk[g * 64:(g + 1) * 64, l, :] = sinks[l, g * 4:(g + 1) * 4][None, :]
    p[:, P_SK:P_SK + 8] = sk.reshape(128, 8)
    p[:, P_AL:P_AL + 8] = np.broadcast_to(a_log.reshape(1, 8), (128, 8))
    p[:, P_DT:P_DT + 8] = np.broadcast_to(dt_bias.reshape(1, 8), (128, 8))
    p[0:32, P_RB:P_RB + 8] = rel_bias
    return p


_NC_CACHE = {}


def kernel(x, norm_w, w_in, conv_w, a_log, dt_bias, lb_param, norm_a, norm_b, sinks, rel_bias,
           w_branch, w_out, final_norm):
    f = lambda a: np.asarray(a, np.float32)
    x = f(x)
    Bn, S, _ = x.shape
    NIT = S // T + 1
    wts = _pack_weights(f(w_in), f(w_branch), f(w_out))
    cst = _consts()
    prms = []
    for role in range(2):
        sel = lambda a: np.stack([f(a)[role], f(a)[role]])
        p = _params(sel(norm_w), sel(conv_w), sel(a_log), sel(dt_bias), f(lb_param), sel(norm_a), sel(norm_b),
                    sel(sinks), f(rel_bias), f(final_norm))
        p[:, P_ROLE] = float(role)
        p[:, P_NROLE] = -float(role)
        fc = np.ones(NIT, np.float32)
        fc[0:role + 1] = 0.0
        p[:, P_FC:P_FC + NIT] = fc[None, :]
        prms.append(p)
    if S not in _NC_CACHE:
        _NC_CACHE[S] = build(S)
    nc = _NC_CACHE[S]
    in_maps = []
    for c in range(8):
        b, role = c // 2, c % 2
        xt = np.zeros((8, 128, S + T), np.float32)
        if role == 0:
            xt[:, :, :S] = np.ascontiguousarray(x[b].T).reshape(8, 128, S)
        in_maps.append({"xT": xt, "wts": np.ascontiguousarray(wts[role]), "cst": cst, "prm": prms[role]})
    res = run_bass_kernel_spmd(nc, in_maps, core_ids=list(range(8)))
    out = np.zeros((Bn, S, D), np.float32)
    for b in range(Bn):
        o = res.results[2 * b + 1]["outT"][:, :, T:S + T]
        out[b] = o.reshape(D, S).T
    return out
```

```python
import math
import os
from contextlib import ExitStack
import numpy as np
import concourse.bass as bass
import concourse.mybir as mybir
from concourse.bass_utils import run_bass_kernel_spmd

F32 = mybir.dt.float32
BF16 = mybir.dt.bfloat16
F32R = mybir.dt.float32r
AF = mybir.ActivationFunctionType
ALU = mybir.AluOpType
AX = mybir.AxisListType

D = 1024
DEPTH = 2
EPS = 1e-6
NW = 88128
T = 512
NEG = -30000.0

C_ID, C_ONES, C_OBLK, C_NOBLK, C_U, C_UM, C_NUM, C_R, C_MA, C_NEGS, C_SEL0, C_SEL1 = [i * 128 for i in range(12)]
C_MC = 12 * 128
C_OH = C_MC + 256
NCST = C_OH + 384
P_NW, P_CW, P_LBF, P_NA, P_NB, P_SK, P_AL, P_DT, P_RB, P_ROLE, P_NROLE, P_FC, P_LBT = 0, 24, 120, 128, 384, 640, 648, 656, 664, 672, 673, 674, 704
NPRM_S = 704
NPRM = 704 + 1024


class Tile:
    def __init__(self, h):
        self.h = h
        self.w = {}
        self.r = {}
        self.dsem = None
        self.dcnt = 0
        self.r32 = False

    def __getitem__(self, idx):
        return View(self, self.h[idx])

    def v(self, ap):
        return View(self, ap)


class View:
    def __init__(self, tile, ap):
        self.tile = tile
        self.ap = ap


def _o(v):
    if v.tile.r32 and v.ap.dtype == F32:
        return v.ap.bitcast(F32R)
    return v.ap


class Prog:
    def __init__(self, nc, es):
        self.nc = nc
        self.es = es
        self.eng = {}
        for name in ("pe", "act", "dve", "pool", "sp"):
            sem = es.enter_context(nc.semaphore("s_" + name))
            self.eng[name] = dict(sem=sem, cnt=0, ops=[], waited={})
        self.final_waits = []

    def _deps(self, eng, reads, writes):
        E = self.eng[eng]
        waits = {}

        def need(key, ent):
            sem, val, who = ent
            if who == eng and eng in ("pe", "sp"):
                return
            if E["waited"].get(key, 0) >= val:
                return
            if key not in waits or waits[key][1] < val:
                waits[key] = (sem, val)

        for v in reads:
            for k, ent in v.tile.w.items():
                need(k, ent)
        for v in writes:
            for k, ent in v.tile.w.items():
                need(k, ent)
            for k, ent in v.tile.r.items():
                need(k, ent)
        for k, (sem, val) in waits.items():
            E["waited"][k] = val
        return list(waits.values())

    def op(self, eng, fn, reads, writes):
        E = self.eng[eng]
        waits = self._deps(eng, reads, writes)
        E["cnt"] += 1
        idx = E["cnt"]
        E["ops"].append((waits, fn, (E["sem"], 1)))
        key = id(E["sem"])
        for v in reads:
            v.tile.r[key] = (E["sem"], idx, eng)
        for v in writes:
            v.tile.w[key] = (E["sem"], idx, eng)

    def dma_in(self, out_v, in_ap, src_tile=None):
        t = out_v.tile
        if t.dsem is None:
            t.dsem = self.es.enter_context(self.nc.semaphore())
        rd = [View(src_tile, None)] if src_tile is not None else []
        waits = self._deps("sp", rd, [out_v])
        t.dcnt += 16
        oap = out_v.ap
        self.eng["sp"]["ops"].append((waits, lambda e: e.dma_start(out=oap, in_=in_ap), (t.dsem, 16)))
        t.w[id(t.dsem)] = (t.dsem, t.dcnt, "dma")
        if src_tile is not None:
            src_tile.r[id(t.dsem)] = (t.dsem, t.dcnt, "dma")

    def collective(self, src_tile, dst_tile, fn):
        if getattr(self, "ccsem", None) is None:
            self.ccsem = self.es.enter_context(self.nc.semaphore("ccsem"))
            self.cccnt = 0
        waits = self._deps("pool", [View(src_tile, None)], [View(dst_tile, None)])
        self.cccnt += 1
        self.eng["pool"]["ops"].append((waits, fn, (self.ccsem, 1)))
        k = id(self.ccsem)
        src_tile.r[k] = (self.ccsem, self.cccnt, "cc")
        dst_tile.w[k] = (self.ccsem, self.cccnt, "cc")

    def dma_cast(self, out_v, in_ap, src_tile=None):
        t = out_v.tile
        if t.dsem is None:
            t.dsem = self.es.enter_context(self.nc.semaphore())
        rd = [View(src_tile, None)] if src_tile is not None else []
        waits = self._deps("pool", rd, [out_v])
        t.dcnt += 16
        oap = out_v.ap
        self.eng["pool"]["ops"].append((waits, lambda e: e.dma_start(out=oap, in_=in_ap), (t.dsem, 16)))
        t.w[id(t.dsem)] = (t.dsem, t.dcnt, "dma")

    def dma_out(self, out_ap, in_v, final=False, dst_tile=None):
        t = in_v.tile
        if t.dsem is None:
            t.dsem = self.es.enter_context(self.nc.semaphore())
        wr = [View(dst_tile, None)] if dst_tile is not None else []
        waits = self._deps("sp", [in_v], wr)
        if dst_tile is not None:
            dst_tile.w[id(t.dsem)] = (t.dsem, t.dcnt + 16, "dma")
        t.dcnt += 16
        iap = in_v.ap
        self.eng["sp"]["ops"].append((waits, lambda e: e.dma_start(out=out_ap, in_=iap), (t.dsem, 16)))
        t.r[id(t.dsem)] = (t.dsem, t.dcnt, "dma")
        if final:
            self.final_waits.append((t.dsem, t.dcnt))

    def replay(self, name, e):
        for waits, fn, inc in self.eng[name]["ops"]:
            for sem, val in waits:
                e.wait_ge(sem, val)
            ins = fn(e)
            ins.then_inc(inc[0], inc[1])
        if name == "sp":
            for sem, val in self.final_waits:
                e.wait_ge(sem, val)

    def mm(self, out, lhsT, rhs, start=True, stop=True):
        o, l, r = out.ap, lhsT.ap, rhs.ap
        self.op("pe", lambda e: e.matmul(o, lhsT=l, rhs=r, start=start, stop=stop), [lhsT, rhs], [out])

    def tr(self, out, in_, ident):
        o, i, d = out.ap, in_.ap, ident.ap
        self.op("pe", lambda e: e.transpose(out=o, in_=i, identity=d), [in_, ident], [out])

    def act(self, out, in_, func, scale=1.0, bias=0.0, eng="act"):
        o, i = _o(out), in_.ap
        rd = [in_]
        sc, bi = scale, bias
        if isinstance(scale, View):
            rd.append(scale)
            sc = scale.ap
        if isinstance(bias, View):
            rd.append(bias)
            bi = bias.ap
        self.op("act", lambda e: e.activation(out=o, in_=i, func=func, bias=bi, scale=sc), rd, [out])

    def tt(self, out, in0, in1, op, eng="dve"):
        o, a, b = _o(out), in0.ap, in1.ap
        self.op(eng, lambda e: e.tensor_tensor(out=o, in0=a, in1=b, op=op), [in0, in1], [out])

    def ts(self, out, in0, s1, op0, s2=None, op1=None, eng="dve"):
        o, a = _o(out), in0.ap
        rd = [in0]
        x1, x2 = s1, s2
        if isinstance(s1, View):
            rd.append(s1)
            x1 = s1.ap
        if isinstance(s2, View):
            rd.append(s2)
            x2 = s2.ap
        if op1 is None:
            self.op(eng, lambda e: e.tensor_scalar(out=o, in0=a, scalar1=x1, scalar2=None, op0=op0), rd, [out])
        else:
            self.op(eng, lambda e: e.tensor_scalar(out=o, in0=a, scalar1=x1, scalar2=x2, op0=op0, op1=op1), rd, [out])

    def stt(self, out, in0, scalar, op0, in1, op1):
        o, a, b = _o(out), in0.ap, in1.ap
        rd = [in0, in1]
        s = scalar
        if isinstance(scalar, View):
            rd.append(scalar)
            s = scalar.ap
        self.op("dve", lambda e: e.scalar_tensor_tensor(out=o, in0=a, scalar=s, in1=b, op0=op0, op1=op1), rd, [out])

    def cp(self, out, in_, eng="dve"):
        o, i = _o(out), in_.ap
        if eng == "act":
            self.op("act", lambda e: e.activation(out=o, in_=i, func=AF.Copy), [in_], [out])
        else:
            self.op(eng, lambda e: e.tensor_copy(out=o, in_=i), [in_], [out])

    def recip(self, out, in_):
        o, i = out.ap, in_.ap
        self.op("dve", lambda e: e.reciprocal(out=o, in_=i), [in_], [out])

    def cpred(self, out, mask, data):
        o, m, d = out.ap, mask.ap, data.ap
        self.op("dve", lambda e: e.copy_predicated(out=o, mask=m, data=d), [out, mask, data], [out])

    def memset(self, out, val, eng="pool"):
        o = out.ap
        self.op(eng, lambda e: e.memset(o, val), [], [out])

    def reduce_sum(self, out, in_):
        o, i = out.ap, in_.ap
        self.op("dve", lambda e: e.tensor_reduce(out=o, in_=i, axis=AX.X, op=ALU.add), [in_], [out])


def build(S):
    nc = bass.Bass("TRN2", target_bir_lowering=False)
    NST = S // T
    NIT = NST + 1
    xT_d = nc.dram_tensor("xT", [8, 128, S + T], F32, kind="ExternalInput").ap()
    w_d = nc.dram_tensor("wts", [128, NW], F32, kind="ExternalInput").ap()
    cst_d = nc.dram_tensor("cst", [128, NCST], F32, kind="ExternalInput").ap()
    prm_d = nc.dram_tensor("prm", [128, NPRM], F32, kind="ExternalInput").ap()
    out_d = nc.dram_tensor("outT", [8, 128, S + T], F32, kind="ExternalOutput").ap()
    wbf_d = nc.dram_tensor("wbf", [128, NW], BF16, kind="Internal").ap()
    src_d = nc.dram_tensor("xsrc", [1024, T], F32, kind="Internal").ap()
    dst_d = nc.dram_tensor("xdst", [2048, T], F32, kind="Internal").ap()
    tbs_h = nc.dram_tensor("tbs", [128, 8, 384], F32, kind="Internal")
    tbs_d = tbs_h.ap()

    with ExitStack() as es:
        P = Prog(nc, es)

        def sb(name, shape, dt=F32):
            return Tile(es.enter_context(nc.sbuf_tensor(name, shape, dt)))

        def ps(name, shape, dt=F32):
            return Tile(es.enter_context(nc.psum_tensor(name, shape, dt)))

        xT = sb("xT_s", [128, 8, T])
        hT = sb("hT", [128, 8, T], BF16)
        yT = sb("yT", [128, 12, T], BF16)
        mT = sb("mT", [128, 8, T], BF16)
        RA = sb("RA", [128, 16448], BF16)
        RB = sb("RB", [128, 17408], BF16)
        CST = sb("CST", [128, NCST])
        PRM = sb("PRM", [128, NPRM_S])
        identb = sb("identb", [128, 128], BF16)
        onesr = sb("onesr", [128, 128])
        onesr.r32 = True
        onesb = sb("onesb", [128, 128], BF16)
        EB = sb("EB", [128, 2, 8, 128], BF16)
        omlT = sb("omlT", [128, 2, 4])
        omlB = sb("omlB", [128, 1, 512])
        negeal = sb("negeal", [128, 2, 4])
        esink = sb("esink", [128, 2, 4])
        SA = sb("SA", [128, 1, 4, 128])
        SAb = sb("SAb", [128, 1, 4, 128], BF16)
        SBs = sb("SBs", [128, 1, 4, 128])
        SBb = sb("SBb", [128, 1, 4, 128], BF16)
        pc = sb("pc", [128, 1, 12, 131], BF16)
        kTc = sb("kTc", [128, 1, 2, 128], BF16)
        vC = sb("vC", [128, 1, 2, 128], BF16)
        dg = sb("dg", [128, 12, 4, 128], BF16)
        fA = [sb(f"fA{i}", [128, 512]) for i in range(10)]
        fA_role = sb("rolem", [128, 512])
        srcT, dstT = Tile(None), Tile(None)
        bA = [sb(f"bA{i}", [128, 512], BF16) for i in range(6)]
        sm = [sb(f"sm{i}", [128, 16]) for i in range(9)]
        fB = [sb(f"fB{i}", [128, 512]) for i in range(10)]
        for t_ in fB:
            t_.r32 = True
        bB = [sb(f"bB{i}", [128, 512], BF16) for i in range(6)]
        PB = [ps(f"pb{i}", [128, 512]) for i in range(8)]

        class Rot:
            def __init__(self, banks):
                self.banks, self.i = banks, 0

            def __call__(self):
                self.i += 1
                return self.banks[self.i % len(self.banks)]
        rotA, rotB = Rot([PB[2], PB[3]]), Rot([PB[5], PB[6], PB[7]])
        rotAll = Rot([PB[2], PB[3], PB[5], PB[6], PB[7]])
        pbank = rotAll
        L0, L1 = PB[0], PB[1]

        def Rv(v):
            return v.tile.v(v.ap.bitcast(F32R))

        def cs(off, n=128, rows=128):
            return CST[0:rows, off:off + n]

        ident = cs(C_ID)

        P.dma_in(CST[:, :], cst_d)
        P.dma_in(PRM[:, :], prm_d[:, 0:NPRM_S])
        P.dma_in(fA[7][:, :], prm_d[:, P_LBT:P_LBT + 512])
        P.dma_in(fA[8][:, :], prm_d[:, P_LBT + 512:P_LBT + 1024])
        P.cp(identb[:, :], cs(C_ID))
        P.cp(onesb[:, :], cs(C_ONES))
        P.cp(onesr[:, :], cs(C_ONES))
        for t_ in (SA, SBs):
            P.memset(t_[:, :, :, :], 0.0)
        for t_ in (SAb, SBb):
            P.memset(t_[:, :, :, :], 0.0)
        P.memset(pc[:, :, :, :], 0.0)
        P.memset(kTc[:, :, :, :], 0.0)
        P.memset(vC[:, :, :, :], 0.0)
        P.tt(fA[0][:, :], fA[8][:, :], fA[7][:, :], ALU.subtract)
        P.act(fA[1][:, :], fA[0][:, :], AF.Sigmoid)
        P.ts(omlB[:, 0, :], fA[1][:, :], PRM[:, P_NROLE:P_NROLE + 1], ALU.mult, 1.0, ALU.add)
        lbf = PRM[:, P_LBF:P_LBF + 8].ap.rearrange("p (l c) -> p l c", l=2)
        P.tt(sm[0][:, 0:4], PRM.v(lbf[:, 1, :]), PRM.v(lbf[:, 0, :]), ALU.subtract)
        P.act(sm[0][:, 4:8], sm[0][:, 0:4], AF.Sigmoid)
        P.ts(omlT[:, 0, :], sm[0][:, 4:8], PRM[:, P_NROLE:P_NROLE + 1], ALU.mult, 1.0, ALU.add)
        rolem = fA_role
        P.cp(rolem[:, :], PRM.v(PRM[:, P_ROLE:P_ROLE + 1].ap.broadcast_to([128, 512])))
        P.act(negeal[:, :, :], PRM.v(PRM[:, P_AL:P_AL + 8].ap.rearrange("p (l c) -> p l c", l=2)), AF.Exp)
        P.ts(negeal[:, :, :], negeal[:, :, :], -1.0, ALU.mult)
        P.act(esink[:, :, :], PRM.v(PRM[:, P_SK:P_SK + 8].ap.rearrange("p (l c) -> p l c", l=2)), AF.Exp)
        RAf = RA.v(RA[:, 0:6144].ap.bitcast(F32).rearrange("p (h c) -> p h c", h=8))
        RAg = RA.v(RA[:, 6144:12288].ap.bitcast(F32).rearrange("p (h c) -> p h c", h=8))
        for h in range(8):
            P.ts(RA.v(RAg.ap[0:32, h, :]), CST[0:32, C_OH:C_OH + 384], PRM[0:32, P_RB + h:P_RB + h + 1], ALU.mult)
        for h in range(8):
            pb = pbank()
            P.mm(pb[:, 0:384], CST[0:32, C_ONES:C_ONES + 128], RA.v(RAg.ap[0:32, h, :]))
            P.cp(RA.v(RAf.ap[:, h, :]), pb[:, 0:384], eng="act")
        P.dma_out(tbs_d, RAf)
        EBf = RB.v(RB[:, 0:4096].ap.bitcast(F32).rearrange("p (k h q) -> p k h q", k=2, h=8))
        for kb in range(2):
            src = bass.AP(tensor=tbs_d.tensor, offset=128 * (1 - kb) + 127, ap=[[3071, 128], [384, 8], [1, 128]])
            waits = [(RA.dsem, RA.dcnt)]
            t = RB
            if t.dsem is None:
                t.dsem = es.enter_context(nc.semaphore())
            t.dcnt += 16
            oap = EBf.ap[:, kb, :, :]
            P.eng["sp"]["ops"].append((waits, (lambda e, oap=oap, src=src: e.dma_start(out=oap, in_=src)), (t.dsem, 16)))
            t.w[id(t.dsem)] = (t.dsem, t.dcnt, "dma")
        P.act(EBf, EBf, AF.Exp)
        mc = CST[:, C_MC:C_MC + 256].ap.rearrange("p (k q) -> p k q", k=2)
        for kb in range(2):
            P.tt(EB[:, kb, :, :], RB.v(EBf.ap[:, kb, :, :]), CST.v(mc[:, kb:kb + 1, :].broadcast_to([128, 8, 128])), ALU.mult)

        for ch in range(12):
            for j in range(4):
                c0 = P_CW + ch * 4 + j
                P.ts(dg[:, ch, j, :], cs(C_ID), PRM[:, c0:c0 + 1], ALU.mult)

        wbfT = Tile(None)
        use_f32 = [True]

        def convert_weights():
            CH = 4096
            for off in range(0, NW, CH):
                n_ = min(CH, NW - off)
                P.dma_cast(View(wbfT, wbf_d[:, off:off + n_]), w_d[:, off:off + n_])

        def load(l, src_off, n, dst_view):
            if use_f32[0]:
                P.dma_cast(dst_view, w_d[:, src_off:src_off + n])
            else:
                P.dma_cast(dst_view, wbf_d[:, src_off:src_off + n], src_tile=wbfT)

        OFF_A, OFF_B, OFF_C, OFF_D1, OFF_D2 = 0, 16384, 16384 + 16448, 16384 + 16448 + 10240, 16384 + 16448 + 10240 + 36864

        WA = RA.v(RA[:, 0:16384].ap.rearrange("p (h k c) -> p h k c", h=4, k=8))
        WBq = RB.v(RB[:, 0:12288].ap.rearrange("p (ch k c) -> p ch k c", ch=12, k=8))
        WBz = RB.v(RB[:, 12288:16384].ap.rearrange("p (k c) -> p k c", k=8))
        WBba = RB.v(RB[:, 16384:16448].ap.rearrange("p (k c) -> p k c", k=8))
        WCq = RA.v(RA[:, 0:4096].ap.rearrange("p (j k c) -> p j k c", j=4, k=8))
        WCk = RA.v(RA[:, 4096:5120].ap.rearrange("p (k c) -> p k c", k=8))
        WCv = RA.v(RA[:, 5120:6144].ap.rearrange("p (k c) -> p k c", k=8))
        WCg = RA.v(RA[:, 6144:10240].ap.rearrange("p (j k c) -> p j k c", j=4, k=8))

        def D1slot(slot):
            return (RB, slot * 4608) if slot < 2 else (RA, 10240)

        def WD1(slot):
            R_, base = D1slot(slot)
            g = R_.v(R_[:, base:base + 3072].ap.rearrange("p (n k c) -> p n k c", n=3, k=8))
            b = R_.v(R_[:, base + 3072:base + 4608].ap.rearrange("p (k c) -> p k c", k=12))
            return g, b
        WO = RB.v(RB[:, 9216:17408].ap.rearrange("p (o k c) -> p o k c", o=8, k=8))

        def load_A(l):
            for h in range(4):
                load(l, OFF_A + h * 4096, 4096, RA[:, h * 4096:(h + 1) * 4096])

        def load_B(l):
            for u in range(4):
                load(l, OFF_B + u * 4096, 4096, RB[:, u * 4096:(u + 1) * 4096])
            load(l, OFF_B + 16384, 64, RB[:, 16384:16448])

        def load_C(l):
            load(l, OFF_C, 4096, RA[:, 0:4096])
            load(l, OFF_C + 4096, 4096, RA[:, 4096:8192])
            load(l, OFF_C + 8192, 2048, RA[:, 8192:10240])

        def load_D1(l, dc):
            R_, base = D1slot(dc % 3)
            load(l, OFF_D1 + dc * 4608, 4608, R_[:, base:base + 4608])

        def load_D2(l):
            load(l, OFF_D2, 4096, RB[:, 9216:9216 + 4096])
            load(l, OFF_D2 + 4096, 4096, RB[:, 9216 + 4096:17408])

        def r3(v):
            return v.tile.v(v.ap.rearrange("p (h c) -> p h c", h=4))

        def b3(tl, c0):
            return tl.v(tl[:, c0:c0 + 4].ap.unsqueeze(2).broadcast_to([128, 4, 128]))

        def cb(off):
            return CST.v(CST[:, off:off + 128].ap.unsqueeze(1).broadcast_to([128, 4, 128]))

        def rms_rstd():
            pss = pbank()
            for kc in range(8):
                sq = fB[kc % 2]
                P.act(sq[:, :], xT[:, kc, :], AF.Square)
                P.mm(pss[:, :], Rv(onesr[:, :]), Rv(sq[:, :]), start=(kc == 0), stop=(kc == 7))
            P.act(fA[2][:, :], pss[:, :], AF.Ln, scale=1.0 / D, bias=EPS)
            P.act(fA[3][:, :], fA[2][:, :], AF.Exp, scale=-0.5)
            return fA[3]

        def rmsnorm(wl, dst_fn):
            rs = rms_rstd()
            for kc in range(8):
                P.stt(dst_fn(kc), xT[:, kc, :], PRM[:, P_NW + wl * 8 + kc:P_NW + wl * 8 + kc + 1], ALU.mult,
                      rs[:, :], ALU.mult)

        def gated_store(o_v, gate_ps, nwb_v, ychunk0, tk, F, Bf, smt, rot, gs_done=False):
            osq, t1, gs = F[4], F[5], F[6]
            if not gs_done:
                P.act(gs[:, :], gate_ps, AF.Silu)
            P.act(osq[:, :], o_v, AF.Square)
            P.reduce_sum(smt[:, 0:4], r3(osq[:, :]))
            P.act(smt[:, 4:8], smt[:, 0:4], AF.Ln, scale=1.0 / 128, bias=EPS)
            P.act(smt[:, 8:12], smt[:, 4:8], AF.Exp, scale=-0.5)
            P.tt(r3(t1[:, :]), r3(o_v), b3(smt, 8), ALU.mult)
            P.tt(r3(gs[:, :]), r3(gs[:, :]), nwb_v.tile.v(nwb_v.ap.unsqueeze(1).broadcast_to([128, 4, 128])), ALU.mult)
            yb = Bf[5]
            P.tt(yb[:, :], t1[:, :], gs[:, :], ALU.mult)
            pbt = rot()
            pbv = pbt.v(pbt[:, 0:256].ap.bitcast(BF16))
            for h in range(4):
                P.tr(pbt.v(pbv.ap[:, h * 128:(h + 1) * 128]), yb[:, h * 128:(h + 1) * 128], identb[:, :])
            P.cp(yT[:, ychunk0:ychunk0 + 4, tk * 128:(tk + 1) * 128], r3(pbv), eng="act")

        def branch_A(l):
            rot = rotA
            for tk in range(T // 128):
                tok = slice(tk * 128, (tk + 1) * 128)
                pg, po = L1, L0
                qTs, sgT, sig, negk, logf, e4 = fA[0], fA[1], fA[2], fA[3], fA[7], fA[8]
                ke, vb = bA[0], bA[1]

                def proj_fm(dst, c0):
                    for h in range(4):
                        hs = slice(h * 128, (h + 1) * 128)
                        for kc in range(8):
                            P.mm(dst[:, hs], RA.v(WA.ap[:, h, kc, c0:c0 + 128]), hT[:, kc, tok], start=(kc == 0), stop=(kc == 7))

                def proj_tm(dst, c0):
                    for h in range(4):
                        hs = slice(h * 128, (h + 1) * 128)
                        for kc in range(8):
                            P.mm(dst[:, hs], hT[:, kc, tok], RA.v(WA.ap[:, h, kc, c0:c0 + 128]), start=(kc == 0), stop=(kc == 7))
                ptm0 = rot()
                proj_tm(ptm0, 128)
                P.act(sig[:, :], ptm0[:, :], AF.Sigmoid)
                yield
                pf = rot()
                proj_fm(pf, 128)
                P.act(sgT[:, :], pf[:, :], AF.Sigmoid, scale=-1.0)
                P.stt(negk[:, :], sig[:, :], -1.0, ALU.add, omlB[:, l, :], ALU.mult)
                yield
                pq = rot()
                proj_fm(pq, 0)
                P.act(qTs[:, :], pq[:, :], AF.Silu)
                proj_tm(pg, 384)
                P.act(fA[6][:, :], pg[:, :], AF.Silu)
                P.act(logf[:, :], negk[:, :], AF.Ln, scale=1.0, bias=1.0)
                yield
                ptm1 = rot()
                proj_tm(ptm1, 256)
                P.cp(vb[:, :], ptm1[:, :], eng="act")
                yield
                prev = rot()
                P.mm(prev[:, :], cs(C_R), logf[:, :])
                P.act(e4[:, :], prev[:, :], AF.Exp)
                P.stt(ke[:, :], negk[:, :], -1.0, ALU.mult, e4[:, :], ALU.mult)
                yield
                E1, E2, E3, tmp = fA[9], fA[4], fA[5], fA[2]
                qsT, qhT, khT, scT = bA[2], bA[3], bA[4], bA[5]
                for (Ex, coff) in ((E2, C_UM), (E3, C_NUM), (E1, C_U)):
                    pcx = rot()
                    for h in range(4):
                        hs = slice(h * 128, (h + 1) * 128)
                        P.mm(pcx[:, hs], logf[:, hs], cs(coff))
                    P.act(Ex[:, :], pcx[:, :], AF.Exp)
                    yield
                P.tt(qhT[:, :], qTs[:, :], E2[:, :], ALU.mult)
                P.tt(tmp[:, :], sgT[:, :], E3[:, :], ALU.mult)
                P.tt(r3(khT[:, :]), r3(tmp[:, :]), omlT.v(omlT[:, l, :].ap.unsqueeze(2).broadcast_to([128, 4, 128])), ALU.mult)
                P.tt(qsT[:, :], qTs[:, :], E1[:, :], ALU.mult)
                yield
                psc = rot()
                for h in range(4):
                    hs = slice(h * 128, (h + 1) * 128)
                    P.mm(psc[:, hs], khT[:, hs], qhT[:, hs])
                P.tt(r3(scT[:, :]), r3(psc[:, :]), cb(C_MA), ALU.mult)
                yield
                E13 = E1.v(E1[:, :].ap.rearrange("p (h c) -> p h c", h=4))
                for c in range(2):
                    r = slice(c * 64, (c + 1) * 64)
                    for h in range(4):
                        hs = slice(h * 128, (h + 1) * 128)
                        cc = slice(h * 128 + c * 64, h * 128 + (c + 1) * 64)
                        P.mm(po[r, hs], scT[:, cc], vb[:, hs], start=True, stop=False)
                        P.mm(po[r, hs], qsT[:, cc], SAb[:, l, h, :], start=False, stop=True)
                    pst = rot()
                    for h in range(4):
                        hs = slice(h * 128, (h + 1) * 128)
                        P.mm(pst[:, hs], ke[r, hs], vb[r, hs])
                    ebb = E1.v(E13.ap[:, :, c * 64 + 63:c * 64 + 64].broadcast_to([128, 4, 128]))
                    P.tt(SA[:, l, :, :], SA[:, l, :, :], ebb, ALU.mult)
                    P.tt(SA[:, l, :, :], SA[:, l, :, :], r3(pst[:, :]), ALU.add)
                    P.cp(SAb[:, l, :, :], SA[:, l, :, :], eng="act")
                    yield
                gated_store(po[:, :], pg[:, :], PRM[:, P_NA + l * 128:P_NA + (l + 1) * 128], 0, tk, fA, bA, sm[1], rot, gs_done=True)
                yield

        def branch_B(l):
            rot = rotB
            F, Bf = fB, bB
            for tk in range(T // 128):
                tok = slice(tk * 128, (tk + 1) * 128)
                P.cp(pc[:, l, :, 0:3], pc[:, l, :, 128:131])
                for g3 in range(3):
                    pp = rot()
                    for j4 in range(4):
                        ch = g3 * 4 + j4
                        for kc in range(8):
                            P.mm(pp[:, j4 * 128:(j4 + 1) * 128], RB.v(WBq.ap[:, ch, kc, :]), hT[:, kc, tok], start=(kc == 0), stop=(kc == 7))
                    P.cp(pc[:, l, g3 * 4:(g3 + 1) * 4, 3:131], r3(pp[:, :]), eng="act")
                    yield
                pba, pz = rot(), PB[4]
                for kc in range(8):
                    P.mm(pba[:, 0:8], hT[:, kc, tok], RB.v(WBba.ap[:, kc, :]), start=(kc == 0), stop=(kc == 7))
                g_, beta, eg, ekd, ge01, bg = sm[2], sm[3], sm[4], sm[5], sm[6], sm[7]
                P.act(beta[:, 0:4], pba[:, 0:4], AF.Sigmoid)
                P.tt(g_[:, 4:8], pba[:, 4:8], PRM[:, P_DT + l * 4:P_DT + l * 4 + 4], ALU.add)
                for kc in range(8):
                    P.mm(pz[:, :], hT[:, kc, tok], RB.v(WBz.ap[:, kc, :]), start=(kc == 0), stop=(kc == 7))
                yield
                qs, ks, vT_, sq = F[0], F[1], F[2], F[3]
                for g3, dst in enumerate((qs, ks, vT_)):
                    pp = rot()
                    for j4 in range(4):
                        ch = g3 * 4 + j4
                        for j in range(4):
                            P.mm(pp[:, j4 * 128:(j4 + 1) * 128], dg[:, ch, j, :], pc[:, l, ch, j:j + 128], start=(j == 0), stop=(j == 3))
                    P.act(dst[:, :], pp[:, :], AF.Silu)
                    yield
                P.act(g_[:, 8:12], g_[:, 4:8], AF.Exp)
                P.act(g_[:, 12:16], g_[:, 8:12], AF.Ln, scale=1.0, bias=1.0)
                P.tt(g_[:, 0:4], g_[:, 12:16], negeal[:, l, :], ALU.mult)
                pgc = rot()
                P.mm(pgc[:, 0:4], cs(C_U), g_[:, 0:4])
                P.mm(pgc[:, 4:8], cs(C_OBLK), g_[:, 0:4])
                P.mm(pgc[:, 8:12], cs(C_SEL0), g_[:, 0:4])
                P.mm(pgc[:, 12:16], cs(C_SEL1), g_[:, 0:4])
                P.act(eg[:, 0:4], pgc[:, 0:4], AF.Exp)
                P.act(ge01[:, 0:8], pgc[:, 8:16], AF.Exp)
                P.cp(ekd[:, 4:12], pgc[:, 0:8], eng="act")
                P.tt(ekd[:, 12:16], ekd[:, 8:12], ekd[:, 4:8], ALU.subtract)
                P.act(ekd[:, 0:4], ekd[:, 12:16], AF.Exp)
                P.tt(bg[:, 0:4], beta[:, 0:4], eg[:, 0:4], ALU.mult)
                yield
                qn, kn, qnb = F[4], F[5], Bf[0]
                for (src, dstn, scl) in ((qs, qn, 128.0 ** -0.5), (ks, kn, 1.0)):
                    P.act(sq[:, :], src[:, :], AF.Square)
                    pn = rot()
                    P.mm(pn[:, :], Rv(onesr[:, :]), Rv(sq[:, :]))
                    P.act(sq[:, :], pn[:, :], AF.Ln, scale=1.0, bias=EPS)
                    P.act(sq[:, :], sq[:, :], AF.Exp, scale=-0.5)
                    P.stt(dstn[:, :], src[:, :], scl, ALU.mult, sq[:, :], ALU.mult)
                    yield
                P.cp(qnb[:, :], qn[:, :], eng="act")
                ptk, ptv = rot(), rot()
                for h in range(4):
                    hs = slice(h * 128, (h + 1) * 128)
                    P.tr(ptk[:, hs], kn[:, hs], ident)
                    P.tr(ptv[:, hs], vT_[:, hs], ident)
                vbt, kbg, kd = F[6], F[7], Bf[1]
                P.tt(r3(vbt[:, :]), r3(ptv[:, :]), b3(beta, 0), ALU.mult)
                P.tt(r3(kbg[:, :]), r3(ptk[:, :]), b3(bg, 0), ALU.mult)
                P.tt(r3(kd[:, :]), r3(ptk[:, :]), b3(ekd, 0), ALU.mult)
                yield
                gU, Em = F[8], F[9]
                P.tt(r3(gU[:, :]), cb(C_U), b3(g_, 0), ALU.mult)
                pD, pKK, pQK = rot(), rot(), rot()
                for h in range(4):
                    hs = slice(h * 128, (h + 1) * 128)
                    P.mm(pD[:, hs], gU[:, hs], cs(C_OBLK), start=True, stop=False)
                    P.mm(pD[:, hs], cs(C_NOBLK), gU[:, hs], start=False, stop=False)
                    P.mm(pD[:, hs], ident, cs(C_NEGS), start=False, stop=True)
                for h in range(4):
                    hs = slice(h * 128, (h + 1) * 128)
                    P.mm(pKK[:, hs], Rv(kn[:, hs]), Rv(kn[:, hs]))
                    P.mm(pQK[:, hs], Rv(qn[:, hs]), Rv(kn[:, hs]))
                P.act(Em[:, :], pD[:, :], AF.Exp)
                yield
                Xs, Ys, W = [F[0], F[1]], [F[2], F[3]], F[8]
                P.tt(Xs[0][:, :], pKK[:, :], Em[:, :], ALU.mult)
                P.tt(r3(Xs[0][:, :]), r3(Xs[0][:, :]), b3(beta, 0), ALU.mult)
                P.tt(r3(Em[:, :]), r3(Em[:, :]), cb(C_ID), ALU.add)
                P.tt(Em[:, :], pQK[:, :], Em[:, :], ALU.mult)
                yield
                pt1, pt2 = rot(), rot()
                for h in range(4):
                    hs = slice(h * 128, (h + 1) * 128)
                    P.tr(pt1[:, hs], Xs[0][:, hs], ident)
                    P.tr(pt2[:, hs], Em[:, hs], ident)
                qkT = Bf[2]
                P.cp(Ys[0][:, :], pt1[:, :], eng="act")
                P.cp(qkT[:, :], pt2[:, :], eng="act")
                P.tt(r3(W[:, :]), cb(C_ID), r3(Ys[0][:, :]), ALU.subtract)
                yield
                xi, yi = 0, 0
                for k in range(1, 6):
                    pX = rot()
                    for h in range(4):
                        hs = slice(h * 128, (h + 1) * 128)
                        P.mm(pX[:, hs], Rv(Ys[yi][:, hs]), Rv(Xs[xi][:, hs]))
                    if k < 5:
                        pY = rot()
                        for h in range(4):
                            hs = slice(h * 128, (h + 1) * 128)
                            P.mm(pY[:, hs], Rv(Xs[xi][:, hs]), Rv(Ys[yi][:, hs]))
                    xi = 1 - xi
                    P.cp(Xs[xi][:, :], pX[:, :], eng="act")
                    if k < 5:
                        yi = 1 - yi
                        P.cp(Ys[yi][:, :], pY[:, :], eng="dve")
                    yield
                    pW = rot()
                    for h in range(4):
                        hs = slice(h * 128, (h + 1) * 128)
                        P.mm(pW[:, hs], Rv(Xs[xi][:, hs]), Rv(W[:, hs]))
                    P.tt(W[:, :], W[:, :], pW[:, :], ALU.add)
                    yield
                pu, pwT = rot(), rot()
                for h in range(4):
                    hs = slice(h * 128, (h + 1) * 128)
                    P.mm(pu[:, hs], Rv(W[:, hs]), Rv(vbt[:, hs]))
                    P.mm(pwT[:, hs], Rv(kbg[:, hs]), Rv(W[:, hs]))
                u_, wT_, vnew, tmpo, otok = F[4], Bf[3], Bf[4], F[5], F[7]
                P.cp(u_[:, :], pu[:, :], eng="act")
                P.cp(wT_[:, :], pwT[:, :], eng="dve")
                yield
                for c in range(2):
                    r = slice(c * 64, (c + 1) * 64)
                    pa1 = rot()
                    for h in range(4):
                        hs = slice(h * 128, (h + 1) * 128)
                        cc = slice(h * 128 + c * 64, h * 128 + (c + 1) * 64)
                        P.mm(pa1[r, hs], wT_[:, cc], SBb[:, l, h, :])
                    P.tt(vnew[r, :], u_[r, :], pa1[r, :], ALU.subtract)
                    yield
                    pa2, pa3, pst = rot(), rot(), rot()
                    for h in range(4):
                        hs = slice(h * 128, (h + 1) * 128)
                        cc = slice(h * 128 + c * 64, h * 128 + (c + 1) * 64)
                        P.mm(pa2[r, hs], qnb[:, cc], SBb[:, l, h, :])
                        P.mm(pa3[r, hs], qkT[r, cc], vnew[r, hs])
                        P.mm(pst[:, hs], kd[r, hs], vnew[r, hs])
                    egb = eg.v(eg[r, 0:4].ap.unsqueeze(2).broadcast_to([64, 4, 128]))
                    P.tt(r3(tmpo[r, :]), r3(pa2[r, :]), egb, ALU.mult)
                    P.tt(otok[r, :], tmpo[r, :], pa3[r, :], ALU.add)
                    P.tt(SBs[:, l, :, :], SBs[:, l, :, :], b3(ge01, c * 4), ALU.mult)
                    P.tt(SBs[:, l, :, :], SBs[:, l, :, :], r3(pst[:, :]), ALU.add)
                    P.cp(SBb[:, l, :, :], SBs[:, l, :, :], eng="act")
                    yield
                gated_store(otok[:, :], pz[:, :], PRM[:, P_NB + l * 128:P_NB + (l + 1) * 128], 4, tk, F, Bf, sm[8], rot)
                yield

        def branch_C(l, st):
            rot = rotA
            for tk in range(T // 128):
                tok = slice(tk * 128, (tk + 1) * 128)
                blk = st * (T // 128) + tk
                cur, prv = blk % 2, (blk + 1) % 2
                qTc, gsC = bA[0], fA[0]
                pq = rot()
                for j in range(4):
                    for kc in range(8):
                        P.mm(pq[:, j * 128:(j + 1) * 128], RA.v(WCq.ap[:, j, kc, :]), hT[:, kc, tok], start=(kc == 0), stop=(kc == 7))
                P.cp(qTc[:, :], pq[:, :], eng="act")
                yield
                pkv = rot()
                for kc in range(8):
                    P.mm(pkv[:, 0:128], RA.v(WCk.ap[:, kc, :]), hT[:, kc, tok], start=(kc == 0), stop=(kc == 7))
                for kc in range(8):
                    P.mm(pkv[:, 128:256], hT[:, kc, tok], RA.v(WCv.ap[:, kc, :]), start=(kc == 0), stop=(kc == 7))
                P.cp(kTc[:, l, cur, :], pkv[:, 0:128], eng="act")
                P.cp(vC[:, l, cur, :], pkv[:, 128:256], eng="act")
                yield
                pN, pDn = L0, L1
                for g in range(2):
                    gp = slice(g * 64, (g + 1) * 64)
                    kbs = [(prv, 0), (cur, 1)]
                    for i_kb, (buf, kbi) in enumerate(kbs):
                        psx = rot()
                        P.mm(psx[:, :], kTc[gp, l, buf, :], qTc[gp, :])
                        pe_, pm = fA[1 + (g * 2 + i_kb) % 2], bA[1 + (g * 2 + i_kb) % 2]
                        P.act(pe_[:, :], psx[:, :], AF.Exp, scale=0.125)
                        P.tt(r3(pm[:, :]), r3(pe_[:, :]), EB[:, kbi, g * 4:(g + 1) * 4, :], ALU.mult)
                        if tk == 0 and kbi == 0:
                            P.ts(pm[:, :], pm[:, :], PRM[:, P_FC + st:P_FC + st + 1], ALU.mult)
                        P.mm(pN[gp, :], vC[:, l, buf, g * 64:(g + 1) * 64], pm[:, :], start=(i_kb == 0), stop=(i_kb == len(kbs) - 1))
                        P.mm(pDn[gp, :], onesb[:, 0:64], pm[:, :], start=(i_kb == 0), stop=(i_kb == len(kbs) - 1))
                        yield
                pg = rot()
                for j in range(4):
                    for kc in range(8):
                        P.mm(pg[:, j * 128:(j + 1) * 128], RA.v(WCg.ap[:, j, kc, :]), hT[:, kc, tok], start=(kc == 0), stop=(kc == 7))
                P.act(gsC[:, :], pg[:, :], AF.Silu)
                den, o_ = fA[3], fA[4]
                P.tt(r3(den[:, :]), r3(pDn[:, :]), esink.v(esink[:, l, :].ap.unsqueeze(2).broadcast_to([128, 4, 128])), ALU.add)
                P.act(den[:, :], den[:, :], AF.Ln)
                P.act(den[:, :], den[:, :], AF.Exp, scale=-1.0)
                P.tt(o_[:, :], pN[:, :], den[:, :], ALU.mult)
                P.tt(yT[:, 8:12, tok], r3(o_[:, :]), r3(gsC[:, :]), ALU.mult)
                yield

        def phase_D1(l, dc):
            Wg, Wb = WD1(dc % 3)
            WT = RB if dc % 3 < 2 else RA
            macc, sg, t2 = fA[0], fA[1], fA[2]
            for n in range(3):
                pg, pl = pbank(), pbank()
                for kc in range(8):
                    P.mm(pg[:, :], WT.v(Wg.ap[:, n, kc, :]), hT[:, kc, :], start=(kc == 0), stop=(kc == 7))
                for c in range(4):
                    P.mm(pl[:, :], WT.v(Wb.ap[:, n * 4 + c, :]), yT[:, n * 4 + c, :], start=(c == 0), stop=(c == 3))
                P.act(sg[:, :], pg[:, :], AF.Sigmoid)
                if n == 0:
                    P.tt(macc[:, :], sg[:, :], pl[:, :], ALU.mult)
                elif n == 1:
                    P.tt(t2[:, :], sg[:, :], pl[:, :], ALU.mult)
                    P.tt(macc[:, :], macc[:, :], t2[:, :], ALU.add)
                else:
                    P.tt(t2[:, :], sg[:, :], pl[:, :], ALU.mult)
                    P.tt(mT[:, dc, :], macc[:, :], t2[:, :], ALU.add)

        def phase_D2(l):
            for dp in range(8):
                po = pbank()
                for dc in range(8):
                    P.mm(po[:, :], RB.v(WO.ap[:, dp, dc, :]), mT[:, dc, :], start=(dc == 0), stop=(dc == 7))
                P.tt(xT[:, dp, :], xT[:, dp, :], po[:, :], ALU.add)

        rg = [[0, 1], [2, 3], [4, 5], [6, 7]]
        outb = [fA[8], fA[9]]
        l = 0
        P.memset(fA[0][:, :], 0.0, eng="dve")
        for kc in range(8):
            P.dma_out(src_d[kc * 128:(kc + 1) * 128, :], fA[0][:, :], dst_tile=srcT)
        load_A(0)
        for it in range(NIT):
            P.collective(srcT, dstT, lambda e: e.collective_compute("AllGather", ALU.bypass, replica_groups=rg,
                                                                   ins=[src_d], outs=[dst_d]))
            P.dma_in(xT[:, :, :], xT_d[:, :, it * T:(it + 1) * T].rearrange("k p t -> p k t"))
            for kc in range(8):
                G = fB[2 + kc]
                P.dma_in(G[:, :], dst_d[kc * 128:(kc + 1) * 128, :], src_tile=dstT)
                P.cpred(xT[:, kc, :], rolem.v(rolem[:, :].ap.bitcast(mybir.dt.uint32)), G[:, :])
            rmsnorm(l, lambda kc: hT[:, kc, :])
            load_B(l)
            if it == 0:
                convert_weights()

            def streamA():
                yield from branch_A(l)
                load_C(l)
                yield from branch_C(l, it)
            gA, gB = streamA(), branch_B(l)
            doneA = doneB = False
            nb = 0
            RA_STEPS = int(os.environ.get("K_RA", "1"))
            RB_STEPS = int(os.environ.get("K_RB", "1"))
            while not (doneA and doneB):
                for _ in range(RA_STEPS):
                    if not doneA:
                        try:
                            next(gA)
                        except StopIteration:
                            doneA = True
                for _ in range(RB_STEPS):
                    if not doneB:
                        try:
                            next(gB)
                        except StopIteration:
                            doneB = True
                            load_D1(l, 0)
                            load_D1(l, 1)
            load_D1(l, 2)
            load_D2(l)
            for dc in range(8):
                phase_D1(l, dc)
                if dc + 3 < 8:
                    load_D1(l, dc + 3)
                if dc == 5 and it + 1 < NIT:
                    use_f32[0] = False
                    load_A(l)
            phase_D2(l)
            for kc in range(8):
                P.dma_out(src_d[kc * 128:(kc + 1) * 128, :], xT[:, kc, :], dst_tile=srcT)
            rs = rms_rstd()
            for kc in range(8):
                ob = outb[kc % 2]
                P.stt(ob[:, :], xT[:, kc, :], PRM[:, P_NW + 16 + kc:P_NW + 16 + kc + 1], ALU.mult, rs[:, :], ALU.mult)
                P.dma_out(out_d[kc, :, it * T:(it + 1) * T], ob[:, :], final=True)

        with nc.Block() as block:
            @block.tensor
            def _(e):
                P.replay("pe", e)

            @block.scalar
            def _(e):
                P.replay("act", e)

            @block.vector
            def _(e):
                P.replay("dve", e)

            @block.gpsimd
            def _(e):
                P.replay("pool", e)

            @block.sync
            def _(e):
                P.replay("sp", e)
    return nc


def _t5_bucket_np(dist):
    dist = np.asarray(dist)
    d_f = np.maximum(dist, 1).astype(np.float32)
    large = 16 + (np.log(d_f / np.float32(16)) / np.float32(math.log(128 / 16)) * np.float32(16)).astype(np.int32)
    large = np.minimum(large, 31)
    return np.where(dist < 16, dist, large)


def _consts():
    c = np.zeros((128, NCST), np.float32)
    r = np.arange(128)[:, None]
    t = np.arange(128)[None, :]
    same = (r // 64) == (t // 64)
    c[:, C_ID:C_ID + 128] = np.eye(128)
    c[:, C_ONES:C_ONES + 128] = 1.0
    c[:, C_OBLK:C_OBLK + 128] = same
    c[:, C_NOBLK:C_NOBLK + 128] = -1.0 * same
    U = same & (r <= t)
    Um = same * ((r <= t).astype(np.float32) - ((r % 64) <= 31).astype(np.float32))
    c[:, C_U:C_U + 128] = U
    c[:, C_UM:C_UM + 128] = Um
    c[:, C_NUM:C_NUM + 128] = -Um
    c[:, C_R:C_R + 128] = same & (r > t)
    c[:, C_MA:C_MA + 128] = same & (r <= t)
    c[:, C_NEGS:C_NEGS + 128] = np.where(same & (t < r), 0.0, NEG)
    c[:, C_SEL0:C_SEL0 + 128] = (r // 64 == 0) * np.ones((1, 128))
    c[:, C_SEL1:C_SEL1 + 128] = (r // 64 == 1) * np.ones((1, 128))
    j = np.arange(128)[:, None, None]
    kb = np.arange(2)[None, :, None]
    q = np.arange(128)[None, None, :]
    dist = q + 128 - (kb * 128 + j)
    c[:, C_MC:C_MC + 256] = ((dist >= 0) & (dist < 128)).reshape(128, 256)
    dp = np.arange(384)
    dd = dp - 127
    valid = (dd >= 0) & (dd < 128)
    b = _t5_bucket_np(np.maximum(dd, 0))
    oh = np.zeros((32, 384), np.float32)
    oh[b[valid], dp[valid]] = 1.0
    c[0:32, C_OH:C_OH + 384] = oh
    return c


def _pk(mat):
    n = mat.shape[1]
    return np.ascontiguousarray(mat.reshape(8, 128, n).transpose(1, 0, 2)).reshape(128, 8 * n)


def _pack_weights(w_in, w_branch, w_out):
    out = np.zeros((DEPTH, 128, NW), np.float32)
    for l in range(DEPTH):
        W = w_in[l]
        u = []
        for h in range(4):
            u.append(_pk(np.concatenate([W[:, part * 512 + h * 128: part * 512 + (h + 1) * 128] for part in range(4)], axis=1)))
        for ch in range(12):
            u.append(_pk(W[:, 2048 + ch * 128: 2048 + (ch + 1) * 128]))
        u.append(_pk(W[:, 3584:4096]))
        u.append(_pk(W[:, 4096:4104]))
        pair = lambda base, j: np.concatenate([np.arange(base + j * 64, base + (j + 1) * 64),
                                               np.arange(base + (4 + j) * 64, base + (5 + j) * 64)])
        for j in range(4):
            u.append(_pk(W[:, pair(4104, j)]))
        u.append(_pk(W[:, 4616:4744]))
        u.append(_pk(W[:, 4744:4872]))
        for j in range(4):
            u.append(_pk(W[:, pair(4872, j)]))
        for dc in range(8):
            for n in range(3):
                u.append(_pk(W[:, 5384 + n * 1024 + dc * 128: 5384 + n * 1024 + (dc + 1) * 128]))
            blocks = []
            for n in range(3):
                for c in range(4):
                    if n < 2:
                        rows = np.arange(c * 128, (c + 1) * 128)
                    else:
                        rows = pair(0, c)
                    blocks.append(w_branch[l, n][rows, dc * 128:(dc + 1) * 128])
            u.append(np.concatenate(blocks, axis=1))
        for dp in range(8):
            u.append(_pk(w_out[l][:, dp * 128:(dp + 1) * 128]))
        cat = np.concatenate(u, axis=1)
        assert cat.shape == (128, NW), cat.shape
        out[l] = cat
    return out


def _params(norm_w, conv_w, a_log, dt_bias, lb_param, norm_a, norm_b, sinks, rel_bias, final_norm):
    p = np.zeros((128, NPRM), np.float32)
    nw = np.stack([norm_w[0], norm_w[1], final_norm])
    p[:, P_NW:P_NW + 24] = nw.reshape(3, 8, 128).transpose(2, 0, 1).reshape(128, 24)
    cw = conv_w.reshape(2, 4, 12, 128).transpose(3, 0, 2, 1)
    p[:, P_CW:P_CW + 96] = cw.reshape(128, 96)
    p[:, P_LBT:P_LBT + 1024] = np.broadcast_to(lb_param.reshape(1, 1024), (128, 1024))
    p[:, P_LBF:P_LBF + 8] = lb_param.reshape(2, 4, 128).transpose(2, 0, 1).reshape(128, 8)
    p[:, P_NA:P_NA + 256] = np.broadcast_to(norm_a.reshape(1, 256), (128, 256))
    p[:, P_NB:P_NB + 256] = np.broadcast_to(norm_b.reshape(1, 256), (128, 256))
    sk = np.zeros((128, 2, 4), np.float32)
    for l in range(2):
        for g in range(2):
            sk[g * 64:(g + 1) * 64, l, :] = sinks[l, g * 4:(g + 1) * 4][None, :]
    p[:, P_SK:P_SK + 8] = sk.reshape(128, 8)
    p[:, P_AL:P_AL + 8] = np.broadcast_to(a_log.reshape(1, 8), (128, 8))
    p[:, P_DT:P_DT + 8] = np.broadcast_to(dt_bias.reshape(1, 8), (128, 8))
    p[0:32, P_RB:P_RB + 8] = rel_bias
    return p


_NC_CACHE = {}


def kernel(x, norm_w, w_in, conv_w, a_log, dt_bias, lb_param, norm_a, norm_b, sinks, rel_bias,
           w_branch, w_out, final_norm):
    f = lambda a: np.asarray(a, np.float32)
    x = f(x)
    Bn, S, _ = x.shape
    NIT = S // T + 1
    wts = _pack_weights(f(w_in), f(w_branch), f(w_out))
    cst = _consts()
    prms = []
    for role in range(2):
        sel = lambda a: np.stack([f(a)[role], f(a)[role]])
        p = _params(sel(norm_w), sel(conv_w), sel(a_log), sel(dt_bias), f(lb_param), sel(norm_a), sel(norm_b),
                    sel(sinks), f(rel_bias), f(final_norm))
        p[:, P_ROLE] = float(role)
        p[:, P_NROLE] = -float(role)
        fc = np.ones(NIT, np.float32)
        fc[0:role + 1] = 0.0
        p[:, P_FC:P_FC + NIT] = fc[None, :]
        prms.append(p)
    if S not in _NC_CACHE:
        _NC_CACHE[S] = build(S)
    nc = _NC_CACHE[S]
    in_maps = []
    for c in range(8):
        b, role = c // 2, c % 2
        xt = np.zeros((8, 128, S + T), np.float32)
        if role == 0:
            xt[:, :, :S] = np.ascontiguousarray(x[b].T).reshape(8, 128, S)
        in_maps.append({"xT": xt, "wts": np.ascontiguousarray(wts[role]), "cst": cst, "prm": prms[role]})
    res = run_bass_kernel_spmd(nc, in_maps, core_ids=list(range(8)))
    out = np.zeros((Bn, S, D), np.float32)
    for b in range(Bn):
        o = res.results[2 * b + 1]["outT"][:, :, T:S + T]
        out[b] = o.reshape(D, S).T
    return out
```

```python
import math
import os
from contextlib import ExitStack
import numpy as np
import concourse.bass as bass
import concourse.mybir as mybir
from concourse.bass_utils import run_bass_kernel_spmd

F32 = mybir.dt.float32
BF16 = mybir.dt.bfloat16
F32R = mybir.dt.float32r
AF = mybir.ActivationFunctionType
ALU = mybir.AluOpType
AX = mybir.AxisListType

D = 1024
DEPTH = 2
EPS = 1e-6
NW = 88128
T = 512
NEG = -30000.0

C_ID, C_ONES, C_OBLK, C_NOBLK, C_U, C_UM, C_NUM, C_R, C_MA, C_NEGS, C_SEL0, C_SEL1 = [i * 128 for i in range(12)]
C_MC = 12 * 128
C_OH = C_MC + 256
NCST = C_OH + 384
P_NW, P_CW, P_LBF, P_NA, P_NB, P_SK, P_AL, P_DT, P_RB, P_ROLE, P_NROLE, P_FC, P_LBT = 0, 24, 120, 128, 384, 640, 648, 656, 664, 672, 673, 674, 704
NPRM_S = 704
NPRM = 704 + 1024


class Tile:
    def __init__(self, h):
        self.h = h
        self.w = {}
        self.r = {}
        self.dsem = None
        self.dcnt = 0
        self.r32 = False

    def __getitem__(self, idx):
        return View(self, self.h[idx])

    def v(self, ap):
        return View(self, ap)


class View:
    def __init__(self, tile, ap):
        self.tile = tile
        self.ap = ap


def _o(v):
    if v.tile.r32 and v.ap.dtype == F32:
        return v.ap.bitcast(F32R)
    return v.ap


class Prog:
    def __init__(self, nc, es):
        self.nc = nc
        self.es = es
        self.eng = {}
        for name in ("pe", "act", "dve", "pool", "sp"):
            sem = es.enter_context(nc.semaphore("s_" + name))
            self.eng[name] = dict(sem=sem, cnt=0, ops=[], waited={})
        self.final_waits = []

    def _deps(self, eng, reads, writes):
        E = self.eng[eng]
        waits = {}

        def need(key, ent):
            sem, val, who = ent
            if who == eng and eng in ("pe", "sp"):
                return
            if E["waited"].get(key, 0) >= val:
                return
            if key not in waits or waits[key][1] < val:
                waits[key] = (sem, val)

        for v in reads:
            for k, ent in v.tile.w.items():
                need(k, ent)
        for v in writes:
            for k, ent in v.tile.w.items():
                need(k, ent)
            for k, ent in v.tile.r.items():
                need(k, ent)
        for k, (sem, val) in waits.items():
            E["waited"][k] = val
        return list(waits.values())

    def op(self, eng, fn, reads, writes):
        E = self.eng[eng]
        waits = self._deps(eng, reads, writes)
        E["cnt"] += 1
        idx = E["cnt"]
        E["ops"].append((waits, fn, (E["sem"], 1)))
        key = id(E["sem"])
        for v in reads:
            v.tile.r[key] = (E["sem"], idx, eng)
        for v in writes:
            v.tile.w[key] = (E["sem"], idx, eng)

    def dma_in(self, out_v, in_ap, src_tile=None):
        t = out_v.tile
        if t.dsem is None:
            t.dsem = self.es.enter_context(self.nc.semaphore())
        rd = [View(src_tile, None)] if src_tile is not None else []
        waits = self._deps("sp", rd, [out_v])
        t.dcnt += 16
        oap = out_v.ap
        self.eng["sp"]["ops"].append((waits, lambda e: e.dma_start(out=oap, in_=in_ap), (t.dsem, 16)))
        t.w[id(t.dsem)] = (t.dsem, t.dcnt, "dma")
        if src_tile is not None:
            src_tile.r[id(t.dsem)] = (t.dsem, t.dcnt, "dma")

    def collective(self, src_tile, dst_tile, fn):
        if getattr(self, "ccsem", None) is None:
            self.ccsem = self.es.enter_context(self.nc.semaphore("ccsem"))
            self.cccnt = 0
        waits = self._deps("pool", [View(src_tile, None)], [View(dst_tile, None)])
        self.cccnt += 1
        self.eng["pool"]["ops"].append((waits, fn, (self.ccsem, 1)))
        k = id(self.ccsem)
        src_tile.r[k] = (self.ccsem, self.cccnt, "cc")
        dst_tile.w[k] = (self.ccsem, self.cccnt, "cc")

    def dma_cast(self, out_v, in_ap, src_tile=None):
        t = out_v.tile
        if t.dsem is None:
            t.dsem = self.es.enter_context(self.nc.semaphore())
        rd = [View(src_tile, None)] if src_tile is not None else []
        waits = self._deps("pool", rd, [out_v])
        t.dcnt += 16
        oap = out_v.ap
        self.eng["pool"]["ops"].append((waits, lambda e: e.dma_start(out=oap, in_=in_ap), (t.dsem, 16)))
        t.w[id(t.dsem)] = (t.dsem, t.dcnt, "dma")

    def dma_out(self, out_ap, in_v, final=False, dst_tile=None):
        t = in_v.tile
        if t.dsem is None:
            t.dsem = self.es.enter_context(self.nc.semaphore())
        wr = [View(dst_tile, None)] if dst_tile is not None else []
        waits = self._deps("sp", [in_v], wr)
        if dst_tile is not None:
            dst_tile.w[id(t.dsem)] = (t.dsem, t.dcnt + 16, "dma")
        t.dcnt += 16
        iap = in_v.ap
        self.eng["sp"]["ops"].append((waits, lambda e: e.dma_start(out=out_ap, in_=iap), (t.dsem, 16)))
        t.r[id(t.dsem)] = (t.dsem, t.dcnt, "dma")
        if final:
            self.final_waits.append((t.dsem, t.dcnt))

    def replay(self, name, e):
        for waits, fn, inc in self.eng[name]["ops"]:
            for sem, val in waits:
                e.wait_ge(sem, val)
            ins = fn(e)
            ins.then_inc(inc[0], inc[1])
        if name == "sp":
            for sem, val in self.final_waits:
                e.wait_ge(sem, val)

    def mm(self, out, lhsT, rhs, start=True, stop=True):
        o, l, r = out.ap, lhsT.ap, rhs.ap
        self.op("pe", lambda e: e.matmul(o, lhsT=l, rhs=r, start=start, stop=stop), [lhsT, rhs], [out])

    def tr(self, out, in_, ident):
        o, i, d = out.ap, in_.ap, ident.ap
        self.op("pe", lambda e: e.transpose(out=o, in_=i, identity=d), [in_, ident], [out])

    def act(self, out, in_, func, scale=1.0, bias=0.0, eng="act"):
        o, i = _o(out), in_.ap
        rd = [in_]
        sc, bi = scale, bias
        if isinstance(scale, View):
            rd.append(scale)
            sc = scale.ap
        if isinstance(bias, View):
            rd.append(bias)
            bi = bias.ap
        self.op("act", lambda e: e.activation(out=o, in_=i, func=func, bias=bi, scale=sc), rd, [out])

    def tt(self, out, in0, in1, op, eng="dve"):
        o, a, b = _o(out), in0.ap, in1.ap
        self.op(eng, lambda e: e.tensor_tensor(out=o, in0=a, in1=b, op=op), [in0, in1], [out])

    def ts(self, out, in0, s1, op0, s2=None, op1=None, eng="dve"):
        o, a = _o(out), in0.ap
        rd = [in0]
        x1, x2 = s1, s2
        if isinstance(s1, View):
            rd.append(s1)
            x1 = s1.ap
        if isinstance(s2, View):
            rd.append(s2)
            x2 = s2.ap
        if op1 is None:
            self.op(eng, lambda e: e.tensor_scalar(out=o, in0=a, scalar1=x1, scalar2=None, op0=op0), rd, [out])
        else:
            self.op(eng, lambda e: e.tensor_scalar(out=o, in0=a, scalar1=x1, scalar2=x2, op0=op0, op1=op1), rd, [out])

    def stt(self, out, in0, scalar, op0, in1, op1):
        o, a, b = _o(out), in0.ap, in1.ap
        rd = [in0, in1]
        s = scalar
        if isinstance(scalar, View):
            rd.append(scalar)
            s = scalar.ap
        self.op("dve", lambda e: e.scalar_tensor_tensor(out=o, in0=a, scalar=s, in1=b, op0=op0, op1=op1), rd, [out])

    def cp(self, out, in_, eng="dve"):
        o, i = _o(out), in_.ap
        if eng == "act":
            self.op("act", lambda e: e.activation(out=o, in_=i, func=AF.Copy), [in_], [out])
        else:
            self.op(eng, lambda e: e.tensor_copy(out=o, in_=i), [in_], [out])

    def recip(self, out, in_):
        o, i = out.ap, in_.ap
        self.op("dve", lambda e: e.reciprocal(out=o, in_=i), [in_], [out])

    def cpred(self, out, mask, data):
        o, m, d = out.ap, mask.ap, data.ap
        self.op("dve", lambda e: e.copy_predicated(out=o, mask=m, data=d), [out, mask, data], [out])

    def memset(self, out, val, eng="pool"):
        o = out.ap
        self.op(eng, lambda e: e.memset(o, val), [], [out])

    def reduce_sum(self, out, in_):
        o, i = out.ap, in_.ap
        self.op("dve", lambda e: e.tensor_reduce(out=o, in_=i, axis=AX.X, op=ALU.add), [in_], [out])


def build(S):
    nc = bass.Bass("TRN2", target_bir_lowering=False)
    NST = S // T
    NIT = NST + 1
    xT_d = nc.dram_tensor("xT", [8, 128, S + T], F32, kind="ExternalInput").ap()
    w_d = nc.dram_tensor("wts", [128, NW], F32, kind="ExternalInput").ap()
    cst_d = nc.dram_tensor("cst", [128, NCST], F32, kind="ExternalInput").ap()
    prm_d = nc.dram_tensor("prm", [128, NPRM], F32, kind="ExternalInput").ap()
    out_d = nc.dram_tensor("outT", [8, 128, S + T], F32, kind="ExternalOutput").ap()
    wbf_d = nc.dram_tensor("wbf", [128, NW], BF16, kind="Internal").ap()
    src_d = nc.dram_tensor("xsrc", [1024, T], F32, kind="Internal").ap()
    dst_d = nc.dram_tensor("xdst", [2048, T], F32, kind="Internal").ap()
    tbs_h = nc.dram_tensor("tbs", [128, 8, 384], F32, kind="Internal")
    tbs_d = tbs_h.ap()

    with ExitStack() as es:
        P = Prog(nc, es)

        def sb(name, shape, dt=F32):
            return Tile(es.enter_context(nc.sbuf_tensor(name, shape, dt)))

        def ps(name, shape, dt=F32):
            return Tile(es.enter_context(nc.psum_tensor(name, shape, dt)))

        xT = sb("xT_s", [128, 8, T])
        hT = sb("hT", [128, 8, T], BF16)
        yT = sb("yT", [128, 12, T], BF16)
        mT = sb("mT", [128, 8, T], BF16)
        RA = sb("RA", [128, 16448], BF16)
        RB = sb("RB", [128, 17408], BF16)
        CST = sb("CST", [128, NCST])
        PRM = sb("PRM", [128, NPRM_S])
        identb = sb("identb", [128, 128], BF16)
        onesr = sb("onesr", [128, 128])
        onesr.r32 = True
        onesb = sb("onesb", [128, 128], BF16)
        EB = sb("EB", [128, 2, 8, 128], BF16)
        omlT = sb("omlT", [128, 2, 4])
        omlB = sb("omlB", [128, 1, 512])
        negeal = sb("negeal", [128, 2, 4])
        esink = sb("esink", [128, 2, 4])
        SA = sb("SA", [128, 1, 4, 128])
        SAb = sb("SAb", [128, 1, 4, 128], BF16)
        SBs = sb("SBs", [128, 1, 4, 128])
        SBb = sb("SBb", [128, 1, 4, 128], BF16)
        pc = sb("pc", [128, 1, 12, 131], BF16)
        kTc = sb("kTc", [128, 1, 2, 128], BF16)
        vC = sb("vC", [128, 1, 2, 128], BF16)
        dg = sb("dg", [128, 12, 4, 128], BF16)
        fA = [sb(f"fA{i}", [128, 512]) for i in range(10)]
        fA_role = sb("rolem", [128, 512])
        srcT, dstT = Tile(None), Tile(None)
        bA = [sb(f"bA{i}", [128, 512], BF16) for i in range(6)]
        sm = [sb(f"sm{i}", [128, 16]) for i in range(9)]
        fB = [sb(f"fB{i}", [128, 512]) for i in range(10)]
        for t_ in fB:
            t_.r32 = True
        bB = [sb(f"bB{i}", [128, 512], BF16) for i in range(6)]
        PB = [ps(f"pb{i}", [128, 512]) for i in range(8)]

        class Rot:
            def __init__(self, banks):
                self.banks, self.i = banks, 0

            def __call__(self):
                self.i += 1
                return self.banks[self.i % len(self.banks)]
        rotA, rotB = Rot([PB[2], PB[3]]), Rot([PB[5], PB[6], PB[7]])
        rotAll = Rot([PB[2], PB[3], PB[5], PB[6], PB[7]])
        pbank = rotAll
        L0, L1 = PB[0], PB[1]

        def Rv(v):
            return v.tile.v(v.ap.bitcast(F32R))

        def cs(off, n=128, rows=128):
            return CST[0:rows, off:off + n]

        ident = cs(C_ID)

        P.dma_in(CST[:, :], cst_d)
        P.dma_in(PRM[:, :], prm_d[:, 0:NPRM_S])
        P.dma_in(fA[7][:, :], prm_d[:, P_LBT:P_LBT + 512])
        P.dma_in(fA[8][:, :], prm_d[:, P_LBT + 512:P_LBT + 1024])
        P.cp(identb[:, :], cs(C_ID))
        P.cp(onesb[:, :], cs(C_ONES))
        P.cp(onesr[:, :], cs(C_ONES))
        for t_ in (SA, SBs):
            P.memset(t_[:, :, :, :], 0.0)
        for t_ in (SAb, SBb):
            P.memset(t_[:, :, :, :], 0.0)
        P.memset(pc[:, :, :, :], 0.0)
        P.memset(kTc[:, :, :, :], 0.0)
        P.memset(vC[:, :, :, :], 0.0)
        P.tt(fA[0][:, :], fA[8][:, :], fA[7][:, :], ALU.subtract)
        P.act(fA[1][:, :], fA[0][:, :], AF.Sigmoid)
        P.ts(omlB[:, 0, :], fA[1][:, :], PRM[:, P_NROLE:P_NROLE + 1], ALU.mult, 1.0, ALU.add)
        lbf = PRM[:, P_LBF:P_LBF + 8].ap.rearrange("p (l c) -> p l c", l=2)
        P.tt(sm[0][:, 0:4], PRM.v(lbf[:, 1, :]), PRM.v(lbf[:, 0, :]), ALU.subtract)
        P.act(sm[0][:, 4:8], sm[0][:, 0:4], AF.Sigmoid)
        P.ts(omlT[:, 0, :], sm[0][:, 4:8], PRM[:, P_NROLE:P_NROLE + 1], ALU.mult, 1.0, ALU.add)
        rolem = fA_role
        P.cp(rolem[:, :], PRM.v(PRM[:, P_ROLE:P_ROLE + 1].ap.broadcast_to([128, 512])))
        P.act(negeal[:, :, :], PRM.v(PRM[:, P_AL:P_AL + 8].ap.rearrange("p (l c) -> p l c", l=2)), AF.Exp)
        P.ts(negeal[:, :, :], negeal[:, :, :], -1.0, ALU.mult)
        P.act(esink[:, :, :], PRM.v(PRM[:, P_SK:P_SK + 8].ap.rearrange("p (l c) -> p l c", l=2)), AF.Exp)
        RAf = RA.v(RA[:, 0:6144].ap.bitcast(F32).rearrange("p (h c) -> p h c", h=8))
        RAg = RA.v(RA[:, 6144:12288].ap.bitcast(F32).rearrange("p (h c) -> p h c", h=8))
        for h in range(8):
            P.ts(RA.v(RAg.ap[0:32, h, :]), CST[0:32, C_OH:C_OH + 384], PRM[0:32, P_RB + h:P_RB + h + 1], ALU.mult)
        for h in range(8):
            pb = pbank()
            P.mm(pb[:, 0:384], CST[0:32, C_ONES:C_ONES + 128], RA.v(RAg.ap[0:32, h, :]))
            P.cp(RA.v(RAf.ap[:, h, :]), pb[:, 0:384], eng="act")
        P.dma_out(tbs_d, RAf)
        EBf = RB.v(RB[:, 0:4096].ap.bitcast(F32).rearrange("p (k h q) -> p k h q", k=2, h=8))
        for kb in range(2):
            src = bass.AP(tensor=tbs_d.tensor, offset=128 * (1 - kb) + 127, ap=[[3071, 128], [384, 8], [1, 128]])
            waits = [(RA.dsem, RA.dcnt)]
            t = RB
            if t.dsem is None:
                t.dsem = es.enter_context(nc.semaphore())
            t.dcnt += 16
            oap = EBf.ap[:, kb, :, :]
            P.eng["sp"]["ops"].append((waits, (lambda e, oap=oap, src=src: e.dma_start(out=oap, in_=src)), (t.dsem, 16)))
            t.w[id(t.dsem)] = (t.dsem, t.dcnt, "dma")
        P.act(EBf, EBf, AF.Exp)
        mc = CST[:, C_MC:C_MC + 256].ap.rearrange("p (k q) -> p k q", k=2)
        for kb in range(2):
            P.tt(EB[:, kb, :, :], RB.v(EBf.ap[:, kb, :, :]), CST.v(mc[:, kb:kb + 1, :].broadcast_to([128, 8, 128])), ALU.mult)

        for ch in range(12):
            for j in range(4):
                c0 = P_CW + ch * 4 + j
                P.ts(dg[:, ch, j, :], cs(C_ID), PRM[:, c0:c0 + 1], ALU.mult)

        wbfT = Tile(None)
        use_f32 = [True]

        def convert_weights():
            CH = 4096
            for off in range(0, NW, CH):
                n_ = min(CH, NW - off)
                P.dma_cast(View(wbfT, wbf_d[:, off:off + n_]), w_d[:, off:off + n_])

        def load(l, src_off, n, dst_view):
            if use_f32[0]:
                P.dma_cast(dst_view, w_d[:, src_off:src_off + n])
            else:
                P.dma_cast(dst_view, wbf_d[:, src_off:src_off + n], src_tile=wbfT)

        OFF_A, OFF_B, OFF_C, OFF_D1, OFF_D2 = 0, 16384, 16384 + 16448, 16384 + 16448 + 10240, 16384 + 16448 + 10240 + 36864

        WA = RA.v(RA[:, 0:16384].ap.rearrange("p (h k c) -> p h k c", h=4, k=8))
        WBq = RB.v(RB[:, 0:12288].ap.rearrange("p (ch k c) -> p ch k c", ch=12, k=8))
        WBz = RB.v(RB[:, 12288:16384].ap.rearrange("p (k c) -> p k c", k=8))
        WBba = RB.v(RB[:, 16384:16448].ap.rearrange("p (k c) -> p k c", k=8))
        WCq = RA.v(RA[:, 0:4096].ap.rearrange("p (j k c) -> p j k c", j=4, k=8))
        WCk = RA.v(RA[:, 4096:5120].ap.rearrange("p (k c) -> p k c", k=8))
        WCv = RA.v(RA[:, 5120:6144].ap.rearrange("p (k c) -> p k c", k=8))
        WCg = RA.v(RA[:, 6144:10240].ap.rearrange("p (j k c) -> p j k c", j=4, k=8))

        def D1slot(slot):
            return (RB, slot * 4608) if slot < 2 else (RA, 10240)

        def WD1(slot):
            R_, base = D1slot(slot)
            g = R_.v(R_[:, base:base + 3072].ap.rearrange("p (n k c) -> p n k c", n=3, k=8))
            b = R_.v(R_[:, base + 3072:base + 4608].ap.rearrange("p (k c) -> p k c", k=12))
            return g, b
        WO = RB.v(RB[:, 9216:17408].ap.rearrange("p (o k c) -> p o k c", o=8, k=8))

        def load_A(l):
            for h in range(4):
                load(l, OFF_A + h * 4096, 4096, RA[:, h * 4096:(h + 1) * 4096])

        def load_B(l):
            for u in range(4):
                load(l, OFF_B + u * 4096, 4096, RB[:, u * 4096:(u + 1) * 4096])
            load(l, OFF_B + 16384, 64, RB[:, 16384:16448])

        def load_C(l):
            load(l, OFF_C, 4096, RA[:, 0:4096])
            load(l, OFF_C + 4096, 4096, RA[:, 4096:8192])
            load(l, OFF_C + 8192, 2048, RA[:, 8192:10240])

        def load_D1(l, dc):
            R_, base = D1slot(dc % 3)
            load(l, OFF_D1 + dc * 4608, 4608, R_[:, base:base + 4608])

        def load_D2(l):
            load(l, OFF_D2, 4096, RB[:, 9216:9216 + 4096])
            load(l, OFF_D2 + 4096, 4096, RB[:, 9216 + 4096:17408])

        def r3(v):
            return v.tile.v(v.ap.rearrange("p (h c) -> p h c", h=4))

        def b3(tl, c0):
            return tl.v(tl[:, c0:c0 + 4].ap.unsqueeze(2).broadcast_to([128, 4, 128]))

        def cb(off):
            return CST.v(CST[:, off:off + 128].ap.unsqueeze(1).broadcast_to([128, 4, 128]))

        def rms_rstd():
            pss = pbank()
            for kc in range(8):
                sq = fB[kc % 2]
                P.act(sq[:, :], xT[:, kc, :], AF.Square)
                P.mm(pss[:, :], Rv(onesr[:, :]), Rv(sq[:, :]), start=(kc == 0), stop=(kc == 7))
            P.act(fA[2][:, :], pss[:, :], AF.Ln, scale=1.0 / D, bias=EPS)
            P.act(fA[3][:, :], fA[2][:, :], AF.Exp, scale=-0.5)
            return fA[3]

        def rmsnorm(wl, dst_fn):
            rs = rms_rstd()
            for kc in range(8):
                P.stt(dst_fn(kc), xT[:, kc, :], PRM[:, P_NW + wl * 8 + kc:P_NW + wl * 8 + kc + 1], ALU.mult,
                      rs[:, :], ALU.mult)

        def gated_store(o_v, gate_ps, nwb_v, ychunk0, tk, F, Bf, smt, rot, gs_done=False, gs_tile=None):
            osq, t1, gs = F[4], F[5], F[6]
            if gs_tile is not None:
                gs = gs_tile
            if not gs_done:
                P.act(gs[:, :], gate_ps, AF.Silu)
            P.act(osq[:, :], o_v, AF.Square)
            P.reduce_sum(smt[:, 0:4], r3(osq[:, :]))
            P.act(smt[:, 4:8], smt[:, 0:4], AF.Ln, scale=1.0 / 128, bias=EPS)
            P.act(smt[:, 8:12], smt[:, 4:8], AF.Exp, scale=-0.5)
            P.tt(r3(t1[:, :]), r3(o_v), b3(smt, 8), ALU.mult)
            P.tt(r3(gs[:, :]), r3(gs[:, :]), nwb_v.tile.v(nwb_v.ap.unsqueeze(1).broadcast_to([128, 4, 128])), ALU.mult)
            yb = Bf[5]
            P.tt(yb[:, :], t1[:, :], gs[:, :], ALU.mult)
            pbt = rot()
            pbv = pbt.v(pbt[:, 0:256].ap.bitcast(BF16))
            for h in range(4):
                P.tr(pbt.v(pbv.ap[:, h * 128:(h + 1) * 128]), yb[:, h * 128:(h + 1) * 128], identb[:, :])
            P.cp(yT[:, ychunk0:ychunk0 + 4, tk * 128:(tk + 1) * 128], r3(pbv), eng="act")

        def branch_A(l):
            rot = rotA
            for tk in range(T // 128):
                tok = slice(tk * 128, (tk + 1) * 128)
                pg, po = L1, L0
                qTs, sgT, sig, negk, logf, e4 = fA[0], fA[1], fA[2], fA[3], fA[7], fA[8]
                ke, vb = bA[0], bA[1]

                def proj_fm(dst, c0):
                    for h in range(4):
                        hs = slice(h * 128, (h + 1) * 128)
                        for kc in range(8):
                            P.mm(dst[:, hs], RA.v(WA.ap[:, h, kc, c0:c0 + 128]), hT[:, kc, tok], start=(kc == 0), stop=(kc == 7))

                def proj_tm(dst, c0):
                    for h in range(4):
                        hs = slice(h * 128, (h + 1) * 128)
                        for kc in range(8):
                            P.mm(dst[:, hs], hT[:, kc, tok], RA.v(WA.ap[:, h, kc, c0:c0 + 128]), start=(kc == 0), stop=(kc == 7))
                ptm0 = rot()
                proj_tm(ptm0, 128)
                P.act(sig[:, :], ptm0[:, :], AF.Sigmoid)
                yield
                pf = rot()
                proj_fm(pf, 128)
                P.act(sgT[:, :], pf[:, :], AF.Sigmoid, scale=-1.0)
                P.stt(negk[:, :], sig[:, :], -1.0, ALU.add, omlB[:, l, :], ALU.mult)
                yield
                pq = rot()
                proj_fm(pq, 0)
                P.act(qTs[:, :], pq[:, :], AF.Silu)
                proj_tm(pg, 384)
                P.act(fA[6][:, :], pg[:, :], AF.Silu)
                P.act(logf[:, :], negk[:, :], AF.Ln, scale=1.0, bias=1.0)
                yield
                ptm1 = rot()
                proj_tm(ptm1, 256)
                P.cp(vb[:, :], ptm1[:, :], eng="act")
                yield
                prev = rot()
                P.mm(prev[:, :], cs(C_R), logf[:, :])
                P.act(e4[:, :], prev[:, :], AF.Exp)
                P.stt(ke[:, :], negk[:, :], -1.0, ALU.mult, e4[:, :], ALU.mult)
                yield
                E1, E2, E3, tmp = fA[9], fA[4], fA[5], fA[2]
                qsT, qhT, khT, scT = bA[2], bA[3], bA[4], bA[5]
                for (Ex, coff) in ((E2, C_UM), (E3, C_NUM), (E1, C_U)):
                    pcx = rot()
                    for h in range(4):
                        hs = slice(h * 128, (h + 1) * 128)
                        P.mm(pcx[:, hs], logf[:, hs], cs(coff))
                    P.act(Ex[:, :], pcx[:, :], AF.Exp)
                    yield
                P.tt(qhT[:, :], qTs[:, :], E2[:, :], ALU.mult)
                P.tt(tmp[:, :], sgT[:, :], E3[:, :], ALU.mult)
                P.tt(r3(khT[:, :]), r3(tmp[:, :]), omlT.v(omlT[:, l, :].ap.unsqueeze(2).broadcast_to([128, 4, 128])), ALU.mult)
                P.tt(qsT[:, :], qTs[:, :], E1[:, :], ALU.mult)
                yield
                psc = rot()
                for h in range(4):
                    hs = slice(h * 128, (h + 1) * 128)
                    P.mm(psc[:, hs], khT[:, hs], qhT[:, hs])
                P.tt(r3(scT[:, :]), r3(psc[:, :]), cb(C_MA), ALU.mult)
                yield
                E13 = E1.v(E1[:, :].ap.rearrange("p (h c) -> p h c", h=4))
                for c in range(2):
                    r = slice(c * 64, (c + 1) * 64)
                    for h in range(4):
                        hs = slice(h * 128, (h + 1) * 128)
                        cc = slice(h * 128 + c * 64, h * 128 + (c + 1) * 64)
                        P.mm(po[r, hs], scT[:, cc], vb[:, hs], start=True, stop=False)
                        P.mm(po[r, hs], qsT[:, cc], SAb[:, l, h, :], start=False, stop=True)
                    pst = rot()
                    for h in range(4):
                        hs = slice(h * 128, (h + 1) * 128)
                        P.mm(pst[:, hs], ke[r, hs], vb[r, hs])
                    ebb = E1.v(E13.ap[:, :, c * 64 + 63:c * 64 + 64].broadcast_to([128, 4, 128]))
                    P.tt(SA[:, l, :, :], SA[:, l, :, :], ebb, ALU.mult)
                    P.tt(SA[:, l, :, :], SA[:, l, :, :], r3(pst[:, :]), ALU.add)
                    P.cp(SAb[:, l, :, :], SA[:, l, :, :], eng="act")
                    yield
                gated_store(po[:, :], pg[:, :], PRM[:, P_NA + l * 128:P_NA + (l + 1) * 128], 0, tk, fA, bA, sm[1], rot, gs_done=True)
                yield

        def branch_B(l):
            rot = rotB
            F, Bf = fB, bB
            for tk in range(T // 128):
                tok = slice(tk * 128, (tk + 1) * 128)
                P.cp(pc[:, l, :, 0:3], pc[:, l, :, 128:131])
                for g3 in range(3):
                    pp = rot()
                    for j4 in range(4):
                        ch = g3 * 4 + j4
                        for kc in range(8):
                            P.mm(pp[:, j4 * 128:(j4 + 1) * 128], RB.v(WBq.ap[:, ch, kc, :]), hT[:, kc, tok], start=(kc == 0), stop=(kc == 7))
                    P.cp(pc[:, l, g3 * 4:(g3 + 1) * 4, 3:131], r3(pp[:, :]), eng="act")
                    yield
                pba, pz = rot(), PB[4]
                for kc in range(8):
                    P.mm(pba[:, 0:8], hT[:, kc, tok], RB.v(WBba.ap[:, kc, :]), start=(kc == 0), stop=(kc == 7))
                g_, beta, eg, ekd, ge01, bg = sm[2], sm[3], sm[4], sm[5], sm[6], sm[7]
                P.act(beta[:, 0:4], pba[:, 0:4], AF.Sigmoid)
                P.tt(g_[:, 4:8], pba[:, 4:8], PRM[:, P_DT + l * 4:P_DT + l * 4 + 4], ALU.add)
                for kc in range(8):
                    P.mm(pz[:, :], hT[:, kc, tok], RB.v(WBz.ap[:, kc, :]), start=(kc == 0), stop=(kc == 7))
                yield
                qs, ks, vT_, sq = F[0], F[1], F[2], F[3]
                for g3, dst in enumerate((qs, ks, vT_)):
                    pp = rot()
                    for j4 in range(4):
                        ch = g3 * 4 + j4
                        for j in range(4):
                            P.mm(pp[:, j4 * 128:(j4 + 1) * 128], dg[:, ch, j, :], pc[:, l, ch, j:j + 128], start=(j == 0), stop=(j == 3))
                    P.act(dst[:, :], pp[:, :], AF.Silu)
                    yield
                P.act(Bf[5][:, :], pz[:, :], AF.Silu)
                P.act(g_[:, 8:12], g_[:, 4:8], AF.Exp)
                P.act(g_[:, 12:16], g_[:, 8:12], AF.Ln, scale=1.0, bias=1.0)
                P.tt(g_[:, 0:4], g_[:, 12:16], negeal[:, l, :], ALU.mult)
                pgc = rot()
                P.mm(pgc[:, 0:4], cs(C_U), g_[:, 0:4])
                P.mm(pgc[:, 4:8], cs(C_OBLK), g_[:, 0:4])
                P.mm(pgc[:, 8:12], cs(C_SEL0), g_[:, 0:4])
                P.mm(pgc[:, 12:16], cs(C_SEL1), g_[:, 0:4])
                P.act(eg[:, 0:4], pgc[:, 0:4], AF.Exp)
                P.act(ge01[:, 0:8], pgc[:, 8:16], AF.Exp)
                P.cp(ekd[:, 4:12], pgc[:, 0:8], eng="act")
                P.tt(ekd[:, 12:16], ekd[:, 8:12], ekd[:, 4:8], ALU.subtract)
                P.act(ekd[:, 0:4], ekd[:, 12:16], AF.Exp)
                P.tt(bg[:, 0:4], beta[:, 0:4], eg[:, 0:4], ALU.mult)
                yield
                qn, kn, qnb = F[4], F[5], Bf[0]
                for (src, dstn, scl) in ((qs, qn, 128.0 ** -0.5), (ks, kn, 1.0)):
                    P.act(sq[:, :], src[:, :], AF.Square)
                    pn = rot()
                    P.mm(pn[:, :], Rv(onesr[:, :]), Rv(sq[:, :]))
                    P.act(sq[:, :], pn[:, :], AF.Ln, scale=1.0, bias=EPS)
                    P.act(sq[:, :], sq[:, :], AF.Exp, scale=-0.5)
                    P.stt(dstn[:, :], src[:, :], scl, ALU.mult, sq[:, :], ALU.mult)
                    yield
                P.cp(qnb[:, :], qn[:, :], eng="act")
                ptk, ptv = rot(), rot()
                for h in range(4):
                    hs = slice(h * 128, (h + 1) * 128)
                    P.tr(ptk[:, hs], kn[:, hs], ident)
                    P.tr(ptv[:, hs], vT_[:, hs], ident)
                vbt, kbg, kd = F[6], F[7], Bf[1]
                P.tt(r3(vbt[:, :]), r3(ptv[:, :]), b3(beta, 0), ALU.mult)
                P.tt(r3(kbg[:, :]), r3(ptk[:, :]), b3(bg, 0), ALU.mult)
                P.tt(r3(kd[:, :]), r3(ptk[:, :]), b3(ekd, 0), ALU.mult)
                yield
                gU, Em = F[8], F[9]
                P.tt(r3(gU[:, :]), cb(C_U), b3(g_, 0), ALU.mult)
                pD, pKK, pQK = rot(), rot(), rot()
                for h in range(4):
                    hs = slice(h * 128, (h + 1) * 128)
                    P.mm(pD[:, hs], gU[:, hs], cs(C_OBLK), start=True, stop=False)
                    P.mm(pD[:, hs], cs(C_NOBLK), gU[:, hs], start=False, stop=False)
                    P.mm(pD[:, hs], ident, cs(C_NEGS), start=False, stop=True)
                for h in range(4):
                    hs = slice(h * 128, (h + 1) * 128)
                    P.mm(pKK[:, hs], Rv(kn[:, hs]), Rv(kn[:, hs]))
                    P.mm(pQK[:, hs], Rv(qn[:, hs]), Rv(kn[:, hs]))
                P.act(Em[:, :], pD[:, :], AF.Exp)
                yield
                Xs, Ys, W = [F[0], F[1]], [F[2], F[3]], F[8]
                P.tt(Xs[0][:, :], pKK[:, :], Em[:, :], ALU.mult)
                P.tt(r3(Xs[0][:, :]), r3(Xs[0][:, :]), b3(beta, 0), ALU.mult)
                P.tt(r3(Em[:, :]), r3(Em[:, :]), cb(C_ID), ALU.add)
                P.tt(Em[:, :], pQK[:, :], Em[:, :], ALU.mult)
                yield
                pt1, pt2 = rot(), rot()
                for h in range(4):
                    hs = slice(h * 128, (h + 1) * 128)
                    P.tr(pt1[:, hs], Xs[0][:, hs], ident)
                    P.tr(pt2[:, hs], Em[:, hs], ident)
                qkT = Bf[2]
                P.cp(Ys[0][:, :], pt1[:, :], eng="act")
                P.cp(qkT[:, :], pt2[:, :], eng="act")
                P.tt(r3(W[:, :]), cb(C_ID), r3(Ys[0][:, :]), ALU.subtract)
                yield
                xi, yi = 0, 0
                for k in range(1, 6):
                    pX = rot()
                    for h in range(4):
                        hs = slice(h * 128, (h + 1) * 128)
                        P.mm(pX[:, hs], Rv(Ys[yi][:, hs]), Rv(Xs[xi][:, hs]))
                    if k < 5:
                        pY = rot()
                        for h in range(4):
                            hs = slice(h * 128, (h + 1) * 128)
                            P.mm(pY[:, hs], Rv(Xs[xi][:, hs]), Rv(Ys[yi][:, hs]))
                    xi = 1 - xi
                    P.cp(Xs[xi][:, :], pX[:, :], eng="act")
                    if k < 5:
                        yi = 1 - yi
                        P.cp(Ys[yi][:, :], pY[:, :], eng="dve")
                    yield
                    pW = rot()
                    for h in range(4):
                        hs = slice(h * 128, (h + 1) * 128)
                        P.mm(pW[:, hs], Rv(Xs[xi][:, hs]), Rv(W[:, hs]))
                    P.tt(W[:, :], W[:, :], pW[:, :], ALU.add)
                    yield
                pu, pwT = rot(), rot()
                for h in range(4):
                    hs = slice(h * 128, (h + 1) * 128)
                    P.mm(pu[:, hs], Rv(W[:, hs]), Rv(vbt[:, hs]))
                    P.mm(pwT[:, hs], Rv(kbg[:, hs]), Rv(W[:, hs]))
                u_, wT_, vnew, tmpo, otok = F[4], Bf[3], Bf[4], F[5], F[7]
                P.cp(u_[:, :], pu[:, :], eng="act")
                P.cp(wT_[:, :], pwT[:, :], eng="dve")
                yield
                for c in range(2):
                    r = slice(c * 64, (c + 1) * 64)
                    pa1 = rot()
                    for h in range(4):
                        hs = slice(h * 128, (h + 1) * 128)
                        cc = slice(h * 128 + c * 64, h * 128 + (c + 1) * 64)
                        P.mm(pa1[r, hs], wT_[:, cc], SBb[:, l, h, :])
                    P.tt(vnew[r, :], u_[r, :], pa1[r, :], ALU.subtract)
                    yield
                    pa2, pa3, pst = rot(), rot(), rot()
                    for h in range(4):
                        hs = slice(h * 128, (h + 1) * 128)
                        cc = slice(h * 128 + c * 64, h * 128 + (c + 1) * 64)
                        P.mm(pa2[r, hs], qnb[:, cc], SBb[:, l, h, :])
                        P.mm(pa3[r, hs], qkT[r, cc], vnew[r, hs])
                        P.mm(pst[:, hs], kd[r, hs], vnew[r, hs])
                    egb = eg.v(eg[r, 0:4].ap.unsqueeze(2).broadcast_to([64, 4, 128]))
                    P.tt(r3(tmpo[r, :]), r3(pa2[r, :]), egb, ALU.mult)
                    P.tt(otok[r, :], tmpo[r, :], pa3[r, :], ALU.add)
                    P.tt(SBs[:, l, :, :], SBs[:, l, :, :], b3(ge01, c * 4), ALU.mult)
                    P.tt(SBs[:, l, :, :], SBs[:, l, :, :], r3(pst[:, :]), ALU.add)
                    P.cp(SBb[:, l, :, :], SBs[:, l, :, :], eng="act")
                    yield
                gated_store(otok[:, :], pz[:, :], PRM[:, P_NB + l * 128:P_NB + (l + 1) * 128], 4, tk, F, Bf, sm[8], rot, gs_done=True, gs_tile=Bf[5])
                yield

        def branch_C(l, st):
            rot = rotA
            for tk in range(T // 128):
                tok = slice(tk * 128, (tk + 1) * 128)
                blk = st * (T // 128) + tk
                cur, prv = blk % 2, (blk + 1) % 2
                qTc, gsC = bA[0], fA[0]
                pq = rot()
                for j in range(4):
                    for kc in range(8):
                        P.mm(pq[:, j * 128:(j + 1) * 128], RA.v(WCq.ap[:, j, kc, :]), hT[:, kc, tok], start=(kc == 0), stop=(kc == 7))
                P.cp(qTc[:, :], pq[:, :], eng="act")
                yield
                pkv = rot()
                for kc in range(8):
                    P.mm(pkv[:, 0:128], RA.v(WCk.ap[:, kc, :]), hT[:, kc, tok], start=(kc == 0), stop=(kc == 7))
                for kc in range(8):
                    P.mm(pkv[:, 128:256], hT[:, kc, tok], RA.v(WCv.ap[:, kc, :]), start=(kc == 0), stop=(kc == 7))
                P.cp(kTc[:, l, cur, :], pkv[:, 0:128], eng="act")
                P.cp(vC[:, l, cur, :], pkv[:, 128:256], eng="act")
                yield
                pN, pDn = L0, L1
                for g in range(2):
                    gp = slice(g * 64, (g + 1) * 64)
                    kbs = [(prv, 0), (cur, 1)]
                    for i_kb, (buf, kbi) in enumerate(kbs):
                        psx = rot()
                        P.mm(psx[:, :], kTc[gp, l, buf, :], qTc[gp, :])
                        pe_, pm = fA[1 + (g * 2 + i_kb) % 2], bA[1 + (g * 2 + i_kb) % 2]
                        P.act(pe_[:, :], psx[:, :], AF.Exp, scale=0.125)
                        P.tt(r3(pm[:, :]), r3(pe_[:, :]), EB[:, kbi, g * 4:(g + 1) * 4, :], ALU.mult)
                        if tk == 0 and kbi == 0:
                            P.ts(pm[:, :], pm[:, :], PRM[:, P_FC + st:P_FC + st + 1], ALU.mult)
                        P.mm(pN[gp, :], vC[:, l, buf, g * 64:(g + 1) * 64], pm[:, :], start=(i_kb == 0), stop=(i_kb == len(kbs) - 1))
                        P.mm(pDn[gp, :], onesb[:, 0:64], pm[:, :], start=(i_kb == 0), stop=(i_kb == len(kbs) - 1))
                        yield
                pg = rot()
                for j in range(4):
                    for kc in range(8):
                        P.mm(pg[:, j * 128:(j + 1) * 128], RA.v(WCg.ap[:, j, kc, :]), hT[:, kc, tok], start=(kc == 0), stop=(kc == 7))
                P.act(gsC[:, :], pg[:, :], AF.Silu)
                den, o_ = fA[3], fA[4]
                P.tt(r3(den[:, :]), r3(pDn[:, :]), esink.v(esink[:, l, :].ap.unsqueeze(2).broadcast_to([128, 4, 128])), ALU.add)
                P.act(den[:, :], den[:, :], AF.Ln)
                P.act(den[:, :], den[:, :], AF.Exp, scale=-1.0)
                P.tt(o_[:, :], pN[:, :], den[:, :], ALU.mult)
                P.tt(yT[:, 8:12, tok], r3(o_[:, :]), r3(gsC[:, :]), ALU.mult)
                yield

        def phase_D1(l, dc):
            Wg, Wb = WD1(dc % 3)
            WT = RB if dc % 3 < 2 else RA
            macc, sg, t2 = fA[0], fA[1], fA[2]
            for n in range(3):
                pg, pl = pbank(), pbank()
                for kc in range(8):
                    P.mm(pg[:, :], WT.v(Wg.ap[:, n, kc, :]), hT[:, kc, :], start=(kc == 0), stop=(kc == 7))
                for c in range(4):
                    P.mm(pl[:, :], WT.v(Wb.ap[:, n * 4 + c, :]), yT[:, n * 4 + c, :], start=(c == 0), stop=(c == 3))
                P.act(sg[:, :], pg[:, :], AF.Sigmoid)
                if n == 0:
                    P.tt(macc[:, :], sg[:, :], pl[:, :], ALU.mult)
                elif n == 1:
                    P.tt(t2[:, :], sg[:, :], pl[:, :], ALU.mult)
                    P.tt(macc[:, :], macc[:, :], t2[:, :], ALU.add)
                else:
                    P.tt(t2[:, :], sg[:, :], pl[:, :], ALU.mult)
                    P.tt(mT[:, dc, :], macc[:, :], t2[:, :], ALU.add)

        def phase_D2(l):
            for dp in range(8):
                po = pbank()
                for dc in range(8):
                    P.mm(po[:, :], RB.v(WO.ap[:, dp, dc, :]), mT[:, dc, :], start=(dc == 0), stop=(dc == 7))
                P.tt(xT[:, dp, :], xT[:, dp, :], po[:, :], ALU.add)

        rg = [[0, 1], [2, 3], [4, 5], [6, 7]]
        outb = [fA[8], fA[9]]
        l = 0
        P.memset(fA[0][:, :], 0.0, eng="dve")
        for kc in range(8):
            P.dma_out(src_d[kc * 128:(kc + 1) * 128, :], fA[0][:, :], dst_tile=srcT)
        load_A(0)
        for it in range(NIT):
            P.collective(srcT, dstT, lambda e: e.collective_compute("AllGather", ALU.bypass, replica_groups=rg,
                                                                   ins=[src_d], outs=[dst_d]))
            P.dma_in(xT[:, :, :], xT_d[:, :, it * T:(it + 1) * T].rearrange("k p t -> p k t"))
            for kc in range(8):
                G = fB[2 + kc]
                P.dma_in(G[:, :], dst_d[kc * 128:(kc + 1) * 128, :], src_tile=dstT)
                P.cpred(xT[:, kc, :], rolem.v(rolem[:, :].ap.bitcast(mybir.dt.uint32)), G[:, :])
            rmsnorm(l, lambda kc: hT[:, kc, :])
            load_B(l)
            if it == 0:
                convert_weights()

            def streamA():
                yield from branch_A(l)
                load_C(l)
                yield from branch_C(l, it)
            gA, gB = streamA(), branch_B(l)
            doneA = doneB = False
            nb = 0
            RA_STEPS = int(os.environ.get("K_RA", "1"))
            RB_STEPS = int(os.environ.get("K_RB", "1"))
            while not (doneA and doneB):
                for _ in range(RA_STEPS):
                    if not doneA:
                        try:
                            next(gA)
                        except StopIteration:
                            doneA = True
                for _ in range(RB_STEPS):
                    if not doneB:
                        try:
                            next(gB)
                        except StopIteration:
                            doneB = True
                            load_D1(l, 0)
                            load_D1(l, 1)
            load_D1(l, 2)
            load_D2(l)
            for dc in range(8):
                phase_D1(l, dc)
                if dc + 3 < 8:
                    load_D1(l, dc + 3)
                if dc == 5 and it + 1 < NIT:
                    use_f32[0] = False
                    load_A(l)
            phase_D2(l)
            for kc in range(8):
                P.dma_out(src_d[kc * 128:(kc + 1) * 128, :], xT[:, kc, :], dst_tile=srcT)
            rs = rms_rstd()
            for kc in range(8):
                ob = outb[kc % 2]
                P.stt(ob[:, :], xT[:, kc, :], PRM[:, P_NW + 16 + kc:P_NW + 16 + kc + 1], ALU.mult, rs[:, :], ALU.mult)
                P.dma_out(out_d[kc, :, it * T:(it + 1) * T], ob[:, :], final=True)

        with nc.Block() as block:
            @block.tensor
            def _(e):
                P.replay("pe", e)

            @block.scalar
            def _(e):
                P.replay("act", e)

            @block.vector
            def _(e):
                P.replay("dve", e)

            @block.gpsimd
            def _(e):
                P.replay("pool", e)

            @block.sync
            def _(e):
                P.replay("sp", e)
    return nc


def _t5_bucket_np(dist):
    dist = np.asarray(dist)
    d_f = np.maximum(dist, 1).astype(np.float32)
    large = 16 + (np.log(d_f / np.float32(16)) / np.float32(math.log(128 / 16)) * np.float32(16)).astype(np.int32)
    large = np.minimum(large, 31)
    return np.where(dist < 16, dist, large)


def _consts():
    c = np.zeros((128, NCST), np.float32)
    r = np.arange(128)[:, None]
    t = np.arange(128)[None, :]
    same = (r // 64) == (t // 64)
    c[:, C_ID:C_ID + 128] = np.eye(128)
    c[:, C_ONES:C_ONES + 128] = 1.0
    c[:, C_OBLK:C_OBLK + 128] = same
    c[:, C_NOBLK:C_NOBLK + 128] = -1.0 * same
    U = same & (r <= t)
    Um = same * ((r <= t).astype(np.float32) - ((r % 64) <= 31).astype(np.float32))
    c[:, C_U:C_U + 128] = U
    c[:, C_UM:C_UM + 128] = Um
    c[:, C_NUM:C_NUM + 128] = -Um
    c[:, C_R:C_R + 128] = same & (r > t)
    c[:, C_MA:C_MA + 128] = same & (r <= t)
    c[:, C_NEGS:C_NEGS + 128] = np.where(same & (t < r), 0.0, NEG)
    c[:, C_SEL0:C_SEL0 + 128] = (r // 64 == 0) * np.ones((1, 128))
    c[:, C_SEL1:C_SEL1 + 128] = (r // 64 == 1) * np.ones((1, 128))
    j = np.arange(128)[:, None, None]
    kb = np.arange(2)[None, :, None]
    q = np.arange(128)[None, None, :]
    dist = q + 128 - (kb * 128 + j)
    c[:, C_MC:C_MC + 256] = ((dist >= 0) & (dist < 128)).reshape(128, 256)
    dp = np.arange(384)
    dd = dp - 127
    valid = (dd >= 0) & (dd < 128)
    b = _t5_bucket_np(np.maximum(dd, 0))
    oh = np.zeros((32, 384), np.float32)
    oh[b[valid], dp[valid]] = 1.0
    c[0:32, C_OH:C_OH + 384] = oh
    return c


def _pk(mat):
    n = mat.shape[1]
    return np.ascontiguousarray(mat.reshape(8, 128, n).transpose(1, 0, 2)).reshape(128, 8 * n)


def _pack_weights(w_in, w_branch, w_out):
    out = np.zeros((DEPTH, 128, NW), np.float32)
    for l in range(DEPTH):
        W = w_in[l]
        u = []
        for h in range(4):
            u.append(_pk(np.concatenate([W[:, part * 512 + h * 128: part * 512 + (h + 1) * 128] for part in range(4)], axis=1)))
        for ch in range(12):
            u.append(_pk(W[:, 2048 + ch * 128: 2048 + (ch + 1) * 128]))
        u.append(_pk(W[:, 3584:4096]))
        u.append(_pk(W[:, 4096:4104]))
        pair = lambda base, j: np.concatenate([np.arange(base + j * 64, base + (j + 1) * 64),
                                               np.arange(base + (4 + j) * 64, base + (5 + j) * 64)])
        for j in range(4):
            u.append(_pk(W[:, pair(4104, j)]))
        u.append(_pk(W[:, 4616:4744]))
        u.append(_pk(W[:, 4744:4872]))
        for j in range(4):
            u.append(_pk(W[:, pair(4872, j)]))
        for dc in range(8):
            for n in range(3):
                u.append(_pk(W[:, 5384 + n * 1024 + dc * 128: 5384 + n * 1024 + (dc + 1) * 128]))
            blocks = []
            for n in range(3):
                for c in range(4):
                    if n < 2:
                        rows = np.arange(c * 128, (c + 1) * 128)
                    else:
                        rows = pair(0, c)
                    blocks.append(w_branch[l, n][rows, dc * 128:(dc + 1) * 128])
            u.append(np.concatenate(blocks, axis=1))
        for dp in range(8):
            u.append(_pk(w_out[l][:, dp * 128:(dp + 1) * 128]))
        cat = np.concatenate(u, axis=1)
        assert cat.shape == (128, NW), cat.shape
        out[l] = cat
    return out


def _params(norm_w, conv_w, a_log, dt_bias, lb_param, norm_a, norm_b, sinks, rel_bias, final_norm):
    p = np.zeros((128, NPRM), np.float32)
    nw = np.stack([norm_w[0], norm_w[1], final_norm])
    p[:, P_NW:P_NW + 24] = nw.reshape(3, 8, 128).transpose(2, 0, 1).reshape(128, 24)
    cw = conv_w.reshape(2, 4, 12, 128).transpose(3, 0, 2, 1)
    p[:, P_CW:P_CW + 96] = cw.reshape(128, 96)
    p[:, P_LBT:P_LBT + 1024] = np.broadcast_to(lb_param.reshape(1, 1024), (128, 1024))
    p[:, P_LBF:P_LBF + 8] = lb_param.reshape(2, 4, 128).transpose(2, 0, 1).reshape(128, 8)
    p[:, P_NA:P_NA + 256] = np.broadcast_to(norm_a.reshape(1, 256), (128, 256))
    p[:, P_NB:P_NB + 256] = np.broadcast_to(norm_b.reshape(1, 256), (128, 256))
    sk = np.zeros((128, 2, 4), np.float32)
    for l in range(2):
        for g in range(2):
            sk[g * 64:(g + 1) * 64, l, :] = sinks[l, g * 4:(g + 1) * 4][None, :]
    p[:, P_SK:P_SK + 8] = sk.reshape(128, 8)
    p[:, P_AL:P_AL + 8] = np.broadcast_to(a_log.reshape(1, 8), (128, 8))
    p[:, P_DT:P_DT + 8] = np.broadcast_to(dt_bias.reshape(1, 8), (128, 8))
    p[0:32, P_RB:P_RB + 8] = rel_bias
    return p


_NC_CACHE = {}


def kernel(x, norm_w, w_in, conv_w, a_log, dt_bias, lb_param, norm_a, norm_b, sinks, rel_bias,
           w_branch, w_out, final_norm):
    f = lambda a: np.asarray(a, np.float32)
    x = f(x)
    Bn, S, _ = x.shape
    NIT = S // T + 1
    wts = _pack_weights(f(w_in), f(w_branch), f(w_out))
    cst = _consts()
    prms = []
    for role in range(2):
        sel = lambda a: np.stack([f(a)[role], f(a)[role]])
        p = _params(sel(norm_w), sel(conv_w), sel(a_log), sel(dt_bias), f(lb_param), sel(norm_a), sel(norm_b),
                    sel(sinks), f(rel_bias), f(final_norm))
        p[:, P_ROLE] = float(role)
        p[:, P_NROLE] = -float(role)
        fc = np.ones(NIT, np.float32)
        fc[0:role + 1] = 0.0
        p[:, P_FC:P_FC + NIT] = fc[None, :]
        prms.append(p)
    if S not in _NC_CACHE:
        _NC_CACHE[S] = build(S)
    nc = _NC_CACHE[S]
    in_maps = []
    for c in range(8):
        b, role = c // 2, c % 2
        xt = np.zeros((8, 128, S + T), np.float32)
        if role == 0:
            xt[:, :, :S] = np.ascontiguousarray(x[b].T).reshape(8, 128, S)
        in_maps.append({"xT": xt, "wts": np.ascontiguousarray(wts[role]), "cst": cst, "prm": prms[role]})
    res = run_bass_kernel_spmd(nc, in_maps, core_ids=list(range(8)))
    out = np.zeros((Bn, S, D), np.float32)
    for b in range(Bn):
        o = res.results[2 * b + 1]["outT"][:, :, T:S + T]
        out[b] = o.reshape(D, S).T
    return out
```
